# Optimizing a Trainium2 kernel written in Bass

```python
import math
import jax, jax.numpy as jnp
from jax import lax
import numpy as np

D_MODEL = 1024
BATCH = 2
SEQ = 8192
DEPTH = 2
DEC_BATCH = 16
DEC_SEQ = 64
PAST_LEN = 4096

CHUNK = 64
Q_BLOCK = 128
N_EVEN = (DEPTH + 1) // 2
N_ODD = DEPTH // 2
EPS = 1e-6

SB_HEADS = 8
SB_DIM = 64
SB_WIDTH = SB_HEADS * SB_DIM
HG_HEADS = 4
HG_DK = 128
HG_DV = 128
HG_KW = HG_HEADS * HG_DK
HG_VW = HG_HEADS * HG_DV
EVEN_SPLITS = (SB_WIDTH, 2 * SB_WIDTH, 3 * SB_WIDTH, 3 * SB_WIDTH + HG_KW,
               3 * SB_WIDTH + 2 * HG_KW, 3 * SB_WIDTH + 2 * HG_KW + HG_VW)
EVEN_IN = 3 * SB_WIDTH + 2 * HG_KW + 2 * HG_VW
EVEN_OUT = SB_WIDTH + HG_VW
LRU_WIDTH = D_MODEL
LRU_BLOCKS = 8
LRU_BLK = LRU_WIDTH // LRU_BLOCKS
CONV_W = 4
C_SCALE = 8.0
D_FF = -(-8 * D_MODEL // (3 * 256)) * 256
N_MOD = 6

kernel_name = "hybrid_stickbreak_hgrn2_rglru_stream_step"

f32 = jnp.float32


def rms_norm(x, g):
    xf = x.astype(f32)
    return xf * lax.rsqrt(jnp.mean(xf * xf, -1, keepdims=True) + EPS) * g.astype(f32)


def sb_block(q, k, v, q_pos, k_pos):
    z = jnp.einsum('bqhd,bkhd->bhqk', q.astype(f32), k.astype(f32)) / math.sqrt(SB_DIM)
    mask = (k_pos[None, :] < q_pos[:, None])[None, None]
    log_1mb = jnp.where(mask, jax.nn.log_sigmoid(-z), 0.0)
    after = lax.cumsum(log_1mb, axis=3, reverse=True) - log_1mb
    w = jnp.where(mask, jnp.exp(jax.nn.log_sigmoid(z) + after), 0.0)
    return jnp.einsum('bhqk,bkhd->bqhd', w, v.astype(f32))


def sb_prompt(q, k, v):
    B, T, H, d = q.shape
    nb = T // Q_BLOCK
    qb = jnp.moveaxis(q.reshape(B, nb, Q_BLOCK, H, d), 1, 0)
    k_pos = jnp.arange(T)

    def one_block(args):
        qi, i = args
        return sb_block(qi, k, v, i * Q_BLOCK + jnp.arange(Q_BLOCK), k_pos)

    o = lax.map(one_block, (qb, jnp.arange(nb)))
    return jnp.moveaxis(o, 0, 1).reshape(B, T, H, d)


def hgrn_chunk(S, q, k, v, logf):
    L = q.shape[1]
    b = jnp.cumsum(logf, axis=1)
    causal = jnp.tril(jnp.ones((L, L), bool))[None, :, :, None, None]
    decay = jnp.exp(jnp.where(causal, b[:, :, None] - b[:, None, :], -jnp.inf))
    attn = jnp.einsum('bthc,bshc,btshc->bhts', q, k, decay)
    o = jnp.einsum('bhts,bshv->bthv', attn, v) + jnp.einsum('bthc,bhcv->bthv', q * jnp.exp(b), S)
    bL = b[:, -1]
    S_new = jnp.exp(bL)[..., None] * S + jnp.einsum('bshc,bshv->bhcv', k * jnp.exp(bL[:, None] - b), v)
    return S_new, o


def hgrn_prompt(q, k, v, logf):
    B, T, H, dk = q.shape
    n = T // CHUNK
    to_chunks = lambda t: jnp.moveaxis(t.reshape(B, n, CHUNK, H, t.shape[-1]), 1, 0)
    S0 = jnp.zeros((B, H, HG_DK, HG_DV), f32)
    S, o = lax.scan(lambda S, xs: hgrn_chunk(S, *xs), S0,
                    (to_chunks(q), to_chunks(k), to_chunks(v), to_chunks(logf)))
    return S, jnp.moveaxis(o, 0, 1).reshape(B, T, H, HG_DV)


def even_mix(hn, w_in, w_out, g_norm, lb, past_k, past_v, S0):
    B, T, _ = hn.shape
    p = (hn @ w_in).astype(f32)
    q_a, k_a, v_a, q_b, f_b, i_b, g_b = jnp.split(p, EVEN_SPLITS, axis=-1)
    q_a = q_a.reshape(B, T, SB_HEADS, SB_DIM)
    k_a = k_a.reshape(B, T, SB_HEADS, SB_DIM)
    v_a = v_a.reshape(B, T, SB_HEADS, SB_DIM)
    if past_k is None:
        o_a = sb_prompt(q_a, k_a, v_a)
    else:
        P = past_k.shape[1]
        keys = jnp.concatenate([past_k.astype(f32), k_a], axis=1)
        vals = jnp.concatenate([past_v.astype(f32), v_a], axis=1)
        o_a = sb_block(q_a, keys, vals, P + jnp.arange(T), jnp.arange(P + T))
    lbh = lb.reshape(HG_HEADS, HG_DK)
    f = lbh + (1.0 - lbh) * jax.nn.sigmoid(f_b.reshape(B, T, HG_HEADS, HG_DK))
    logf = jnp.log(f)
    k_b = 1.0 - f
    q_b = jax.nn.silu(q_b).reshape(B, T, HG_HEADS, HG_DK) * HG_DK ** -0.5
    i_b = i_b.reshape(B, T, HG_HEADS, HG_DV)
    if S0 is None:
        S, o_b = hgrn_prompt(q_b, k_b, i_b, logf)
    else:
        S, o_b = hgrn_chunk(S0.astype(f32), q_b, k_b, i_b, logf)
    o_b = o_b * lax.rsqrt(jnp.mean(o_b * o_b, -1, keepdims=True) + EPS) * g_norm.astype(f32).reshape(HG_HEADS, HG_DV)
    o_b = o_b.reshape(B, T, HG_VW) * jax.nn.silu(g_b)
    o = jnp.concatenate([o_a.reshape(B, T, SB_WIDTH), o_b], axis=-1).astype(hn.dtype) @ w_out
    return o, k_a.astype(hn.dtype), v_a.astype(hn.dtype), S


def _lin_comb(l, r):
    a_l, b_l = l
    a_r, b_r = r
    return a_l * a_r, a_r * b_l + b_r


def odd_mix(hn, w_in, conv_w, conv_b, wa, ba, wx, bx, lam, w_out, conv_past, h0):
    B, T, _ = hn.shape
    p = (hn @ w_in).astype(f32)
    gate_br, x_br = jnp.split(p, 2, axis=-1)
    pad = jnp.zeros((B, CONV_W - 1, LRU_WIDTH), f32) if conv_past is None else conv_past.astype(f32)
    xp = jnp.concatenate([pad, x_br], axis=1)
    cw = conv_w.astype(f32)
    xc = conv_b.astype(f32) + xp[:, 0:T] * cw[0]
    for j in range(1, CONV_W):
        xc = xc + xp[:, j:j + T] * cw[j]
    xb = xc.reshape(B, T, LRU_BLOCKS, LRU_BLK)
    r = jax.nn.sigmoid(jnp.einsum('bthi,hij->bthj', xb, wa.astype(f32)).reshape(B, T, LRU_WIDTH) + ba)
    gi = jax.nn.sigmoid(jnp.einsum('bthi,hij->bthj', xb, wx.astype(f32)).reshape(B, T, LRU_WIDTH) + bx)
    log_a = C_SCALE * r * jax.nn.log_sigmoid(lam.astype(f32))
    a = jnp.exp(log_a)
    mult = jnp.sqrt(-jnp.expm1(2.0 * log_a))
    if h0 is None:
        mult = jnp.where(jnp.arange(T)[None, :, None] == 0, 1.0, mult)
    u = mult * gi * xc
    if h0 is not None:
        u = u.at[:, 0].add(a[:, 0] * h0.astype(f32))
    _, h = lax.associative_scan(_lin_comb, (a, u), axis=1)
    y = (jax.nn.gelu(gate_br) * h).astype(hn.dtype) @ w_out
    return y, xp[:, -(CONV_W - 1):].astype(hn.dtype), h[:, -1]


def swiglu(h, wg, wu, wd):
    return (jax.nn.silu(h @ wg) * (h @ wu)) @ wd


def trunk(x, c, past_k, past_v, past_S, past_conv, past_h,
          norm_mix, norm_ffn, w_ada, b_ada, w_in_even, w_out_even, hg_gnorm, hg_lb_logits,
          w_in_odd, conv_w, conv_b, lru_wa, lru_ba, lru_wx, lru_bx, lru_lambda, w_out_odd,
          ffn_wg, ffn_wu, ffn_wd, final_norm):
    sample = past_k is not None
    B = x.shape[0]
    lb_all = jnp.cumsum(jax.nn.softmax(hg_lb_logits.astype(f32), axis=0), axis=0)
    sc = jax.nn.silu(c.astype(f32))
    new_k, new_v, new_S, new_conv, new_h = [], [], [], [], []
    for l in range(DEPTH):
        e = l // 2
        m = (sc @ w_ada[l].astype(f32) + b_ada[l]).reshape(B, N_MOD, D_MODEL)[:, :, None, :]
        sh1, sc1, g1, sh2, sc2, g2 = (m[:, i] for i in range(N_MOD))
        hn = (rms_norm(x, norm_mix[l]) * (1.0 + sc1) + sh1).astype(x.dtype)
        if l % 2 == 0:
            o, kn, vn, S = even_mix(hn, w_in_even[e], w_out_even[e], hg_gnorm[e], lb_all[l],
                                    past_k[e] if sample else None,
                                    past_v[e] if sample else None,
                                    past_S[e] if sample else None)
            new_k.append(kn)
            new_v.append(vn)
            new_S.append(S)
        else:
            o, cs, hl = odd_mix(hn, w_in_odd[e], conv_w[e], conv_b[e], lru_wa[e], lru_ba[e],
                                lru_wx[e], lru_bx[e], lru_lambda[e], w_out_odd[e],
                                past_conv[e] if sample else None,
                                past_h[e] if sample else None)
            new_conv.append(cs)
            new_h.append(hl)
        x = x + ((1.0 + g1) * o).astype(x.dtype)
        hn = (rms_norm(x, norm_ffn[l]) * (1.0 + sc2) + sh2).astype(x.dtype)
        x = x + ((1.0 + g2) * swiglu(hn, ffn_wg[l], ffn_wu[l], ffn_wd[l])).astype(x.dtype)
    y = rms_norm(x, final_norm).astype(x.dtype)
    return y, jnp.stack(new_k), jnp.stack(new_v), jnp.stack(new_S), jnp.stack(new_conv), jnp.stack(new_h)


def setup_inputs(seed: int = 0) -> dict:
    key = jax.random.key(seed)
    ks = iter(jax.random.split(key, 32))
    nrm = lambda shape, scale: jax.random.normal(next(ks), shape, f32) * scale
    a0 = jax.random.uniform(next(ks), (N_ODD, LRU_WIDTH), f32, minval=0.9, maxval=0.999)
    p = a0 ** (1.0 / C_SCALE)
    lru_lambda = jnp.log(p) - jnp.log1p(-p)
    return {
        "x_prompt": nrm((BATCH, SEQ, D_MODEL), 1.0),
        "x_sample": nrm((DEC_BATCH, DEC_SEQ, D_MODEL), 1.0),
        "cache_sb_k": nrm((N_EVEN, DEC_BATCH, PAST_LEN, SB_HEADS, SB_DIM), 1.0),
        "cache_sb_v": nrm((N_EVEN, DEC_BATCH, PAST_LEN, SB_HEADS, SB_DIM), 1.0),
        "state_hgrn": nrm((N_EVEN, DEC_BATCH, HG_HEADS, HG_DK, HG_DV), 0.3),
        "state_conv": nrm((N_ODD, DEC_BATCH, CONV_W - 1, LRU_WIDTH), 1.0),
        "state_lru": nrm((N_ODD, DEC_BATCH, LRU_WIDTH), 0.5),
        "c_prompt": nrm((BATCH, D_MODEL), 1.0),
        "c_sample": nrm((DEC_BATCH, D_MODEL), 1.0),
        "norm_mix": 1.0 + nrm((DEPTH, D_MODEL), 0.02),
        "norm_ffn": 1.0 + nrm((DEPTH, D_MODEL), 0.02),
        "w_ada": nrm((DEPTH, D_MODEL, N_MOD * D_MODEL), 0.2 * D_MODEL ** -0.5),
        "b_ada": nrm((DEPTH, N_MOD * D_MODEL), 0.02),
        "w_in_even": nrm((N_EVEN, D_MODEL, EVEN_IN), D_MODEL ** -0.5),
        "w_out_even": nrm((N_EVEN, EVEN_OUT, D_MODEL), EVEN_OUT ** -0.5),
        "hg_gnorm": 1.0 + nrm((N_EVEN, HG_VW), 0.02),
        "hg_lb_logits": nrm((DEPTH + 1, HG_KW), 0.5),
        "w_in_odd": nrm((N_ODD, D_MODEL, 2 * LRU_WIDTH), D_MODEL ** -0.5),
        "conv_w": nrm((N_ODD, CONV_W, LRU_WIDTH), CONV_W ** -0.5),
        "conv_b": nrm((N_ODD, LRU_WIDTH), 0.02),
        "lru_wa": nrm((N_ODD, LRU_BLOCKS, LRU_BLK, LRU_BLK), LRU_BLK ** -0.5),
        "lru_ba": nrm((N_ODD, LRU_WIDTH), 0.02),
        "lru_wx": nrm((N_ODD, LRU_BLOCKS, LRU_BLK, LRU_BLK), LRU_BLK ** -0.5),
        "lru_bx": nrm((N_ODD, LRU_WIDTH), 0.02),
        "lru_lambda": lru_lambda,
        "w_out_odd": nrm((N_ODD, LRU_WIDTH, D_MODEL), LRU_WIDTH ** -0.5),
        "ffn_wg": nrm((DEPTH, D_MODEL, D_FF), D_MODEL ** -0.5),
        "ffn_wu": nrm((DEPTH, D_MODEL, D_FF), D_MODEL ** -0.5),
        "ffn_wd": nrm((DEPTH, D_FF, D_MODEL), D_FF ** -0.5),
        "final_norm": 1.0 + nrm((D_MODEL,), 0.02),
    }


def reference(x_prompt, x_sample, cache_sb_k, cache_sb_v, state_hgrn, state_conv, state_lru,
              c_prompt, c_sample, norm_mix, norm_ffn, w_ada, b_ada, w_in_even, w_out_even,
              hg_gnorm, hg_lb_logits, w_in_odd, conv_w, conv_b, lru_wa, lru_ba, lru_wx, lru_bx,
              lru_lambda, w_out_odd, ffn_wg, ffn_wu, ffn_wd, final_norm):
    weights = (norm_mix, norm_ffn, w_ada, b_ada, w_in_even, w_out_even, hg_gnorm, hg_lb_logits,
               w_in_odd, conv_w, conv_b, lru_wa, lru_ba, lru_wx, lru_bx, lru_lambda, w_out_odd,
               ffn_wg, ffn_wu, ffn_wd, final_norm)
    y_prompt, k_p, v_p, S_p, conv_p, h_p = trunk(x_prompt, c_prompt, None, None, None, None, None, *weights)
    y_sample, k_s, v_s, S_s, conv_s, h_s = trunk(x_sample, c_sample, cache_sb_k, cache_sb_v, state_hgrn,
                                                 state_conv, state_lru, *weights)
    return (y_prompt, y_sample, k_p, v_p, S_p, conv_p, h_p, k_s, v_s, S_s, conv_s, h_s)
```

```python
import numpy as np
from contextlib import ExitStack
import concourse.bass as bass
import concourse.mybir as mybir
from concourse.bass_utils import run_bass_kernel_spmd

F32 = mybir.dt.float32
BF16 = mybir.dt.bfloat16
AF = mybir.ActivationFunctionType
ALU = mybir.AluOpType

NCORE = 8
D = 1024
TOWN = 2048
TPRE = 6144
NSMP = 128
PAST = 4096
DFF = 2816
EPS = 1e-6
NEG = -30000.0
import os
STAGE = int(os.environ.get('KSTAGE', '9'))
SUB = int(os.environ.get('KSUB', '99'))
NPRE = int(os.environ.get('KNPRE', '12'))
SKIP = int(os.environ.get('KSKIP', '0'))
KINDS = os.environ.get('KKINDS', 'pre,own,smp').split(',')
SAME_ENG_SYNC = bool(int(os.environ.get('KSES', '1')))

V_NMIX, V_NFFN, V_BADA, V_GN, V_LB, V_CW, V_CB, V_BA, V_BX, V_LAM, V_FN, V_CT = 0, 16, 32, 128, 132, 144, 176, 184, 192, 200, 208, 216
NV = 248
C_ID, C_MASK, C_CM, C_RS = 0, 128, 128 + 2048, 128 + 2048 + 64
NCST = 128 + 2048 + 64 + 512


NEED_STAGE = {"cache_k": 2, "cache_v": 2, "w_out_even": 3, "ffn_wg": 4, "ffn_wu": 4, "ffn_wd": 4,
              "w_in_odd": 5, "lru_wa": 5, "lru_wx": 5, "w_out_odd": 5, "st_conv": 5, "st_lru": 5}


class Tr:
    def __init__(self, nc):
        self.nc = nc
        self.E = {"pe": nc.tensor, "act": nc.scalar, "dve": nc.vector, "pool": nc.gpsimd, "sp": nc.sync}
        self.nsem = 0
        self.sem = {}
        self.waited = {e: {} for e in self.E}
        self.res = {}
        self.latest = {}
        for e in self.E:
            self._newsem(e)
        self.ring = {}
        self.rpos = {}
        for q in ("sp", "pool", "act"):
            self.ring[q] = [[self._alloc("dq%s%d" % (q, i)), 0] for i in range(12)]
            self.rpos[q] = 0
        self.ninst = 0

    def _alloc(self, name):
        self.nsem += 1
        return (name, self.nc.alloc_semaphore(name))

    def _newsem(self, e):
        self.sem[e] = [self._alloc("e%s%d" % (e, self.nsem)), 0]

    def _wait(self, eng, tok):
        peng, name, sem, val = tok
        if peng == eng:
            if eng == "pe" or not SAME_ENG_SYNC:
                return
        if self.waited[eng].get(name, 0) >= val:
            return
        self.E[eng].wait_ge(sem, val)
        self.ninst += 1
        self.waited[eng][name] = val

    def _deps(self, r, w):
        deps = []
        for x in r:
            st = self.res.get(x)
            if st and st[0]:
                deps.append(st[0])
            if st and x.startswith("ps"):
                deps.extend(st[1].values())
        for x in w:
            st = self.res.get(x)
            if st:
                if st[0]:
                    deps.append(st[0])
                deps.extend(st[1].values())
        return deps

    def _record(self, tok, r, w):
        for x in w:
            self.res[x] = [tok, {}]
        for x in r:
            st = self.res.setdefault(x, [None, {}])
            st[1][tok[1]] = tok
        self.latest[tok[1]] = tok

    def op(self, eng, fn, r=(), w=()):
        for t in self._deps(r, w):
            self._wait(eng, t)
        ins = fn()
        s = self.sem[eng]
        s[1] += 1
        ins.then_inc(s[0][1], 1)
        self.ninst += 1
        tok = (eng, s[0][0], s[0][1], s[1])
        self._record(tok, r, w)
        if s[1] >= 30000:
            self._newsem(eng)
        return tok

    def dma(self, q, out, in_, r=(), w=(), **kw):
        if (SKIP & 1) and any(x in ("kT_scr", "v_scr", "qT_scr") for x in w):
            return
        if (SKIP & 2) and any(x in ("k_out", "v_out") for x in w):
            return
        for t in self._deps(r, w):
            self._wait(q, t)
        slot = self.ring[q][self.rpos[q]]
        self.rpos[q] = (self.rpos[q] + 1) % len(self.ring[q])
        if slot[1] > 0:
            self._wait(q, (None, slot[0][0], slot[0][1], slot[1]))
        ins = self.E[q].dma_start(out=out, in_=in_, **kw)
        slot[1] += 16
        ins.then_inc(slot[0][1], 16)
        self.ninst += 1
        tok = (None, slot[0][0], slot[0][1], slot[1])
        self._record(tok, r, w)
        return tok

    def raw(self, eng, fn, r=(), w=(), inc=1):
        for t in self._deps(r, w):
            self._wait(eng, t)
        ins = fn()
        nm = self._alloc("cc%d" % self.nsem)
        ins.then_inc(nm[1])
        tok = (None, nm[0], nm[1], inc)
        self._record(tok, r, w)
        return tok

    def barrier(self):
        toks = list(self.latest.values())
        for e in self.E:
            for t in toks:
                if t[0] == e:
                    continue
                self._wait(e, t)
        self.res = {}


def build():
    nc = bass.Bass("TRN2", target_bir_lowering=False)
    T = Tr(nc)
    dma = T.dma
    OP = [T.op]

    def op(eng, fn, r=(), w=()):
        return OP[0](eng, fn, r, w)
    PE, ACT, DVE, POOL = "pe", "act", "dve", "pool"
    V, S, G, TE = nc.vector, nc.scalar, nc.gpsimd, nc.tensor

    def din(name, shape):
        if NEED_STAGE.get(name, 0) > STAGE:
            return None
        return nc.dram_tensor(name, list(shape), F32, kind="ExternalInput").ap()

    def dout(name, shape):
        return nc.dram_tensor(name, list(shape), F32, kind="ExternalOutput").ap()

    x_own = din("x_own", [TOWN, D]); x_pre = din("x_pre", [TPRE, D]); x_smp = din("x_smp", [NSMP, D])
    cache_k = din("cache_k", [2, PAST, 512]); cache_v = din("cache_v", [2, PAST, 512])
    st_hgrn = din("st_hgrn", [2, 4, 128, 128]); st_conv = din("st_conv", [2, 128, 24]); st_lru = din("st_lru", [2, 128, 8])
    vec_in = din("vec", [128, NV]); flags_in = din("flags", [128, 32]); cst_in = din("cst", [128, NCST])
    w_ada = din("w_ada", [2, D, 6 * D]); w_in_even = din("w_in_even", [D, 3584]); w_out_even = din("w_out_even", [D, D])
    w_in_odd = din("w_in_odd", [D, 2048]); lru_wa = din("lru_wa", [8, 128, 128]); lru_wx = din("lru_wx", [8, 128, 128])
    w_out_odd = din("w_out_odd", [D, D])
    ffn_wg = din("ffn_wg", [2, D, DFF]); ffn_wu = din("ffn_wu", [2, D, DFF]); ffn_wd = din("ffn_wd", [2, DFF, D])

    y_own = dout("y_own", [TOWN, D]); y_smp = dout("y_smp", [NSMP, D])
    k_own = dout("k_own", [TOWN, 512]); v_own = dout("v_own", [TOWN, 512])
    S_p = dout("S_p", [4, 128, 128]); conv_p = dout("conv_p", [128, 24]); lru_p = dout("lru_p", [128, 8])
    k_smp = dout("k_smp", [NSMP, 512]); v_smp = dout("v_smp", [NSMP, 512])
    S_s = dout("S_s", [2, 4, 128, 128]); conv_s = dout("conv_s", [2, 128, 24]); lru_s = dout("lru_s", [2, 128, 8])

    kT_scr = nc.dram_tensor("kT_scr", [4, 128, TPRE + TOWN], BF16).ap()
    v_scr = nc.dram_tensor("v_scr", [4, 128, 64 * 128], BF16).ap()
    qT_scr = nc.dram_tensor("qT_scr", [4, 128, TOWN], BF16).ap()
    wie_s = nc.dram_tensor("wie_s", [128, 8 * 3584], BF16).ap()
    woe_s = nc.dram_tensor("woe_s", [128, 8 * D], BF16).ap()
    wio_s = nc.dram_tensor("wio_s", [128, 8 * 2048], BF16).ap()
    woo_s = nc.dram_tensor("woo_s", [128, 8 * D], BF16).ap()
    wg_s = nc.dram_tensor("wg_s", [11, 128, 8 * 256], BF16).ap()
    wu_s = nc.dram_tensor("wu_s", [11, 128, 8 * 256], BF16).ap()
    wd_s = nc.dram_tensor("wd_s", [4, 128, 22 * 256], BF16).ap()
    WCACHED = set()

    es = ExitStack()
    AWORDS = 53200
    arena = nc.alloc_sbuf_tensor("arena", [128, AWORDS], F32).ap()
    free_list = [[0, AWORDS]]
    live = {}

    def a_free(name):
        off, n = live.pop(name)
        free_list.append([off, n])
        free_list.sort()
        i = 0
        while i + 1 < len(free_list):
            if free_list[i][0] + free_list[i][1] == free_list[i + 1][0]:
                free_list[i][1] += free_list[i + 1][1]
                del free_list[i + 1]
            else:
                i += 1

    def sb(stack, name, shape, dt):
        shape = list(shape)
        per = 1
        for d_ in shape[1:]:
            per *= d_
        nw = per if dt == F32 else (per + 1) // 2
        nw = (nw + 7) // 8 * 8
        for fl_ in free_list:
            if fl_[1] >= nw:
                off = fl_[0]
                fl_[0] += nw
                fl_[1] -= nw
                break
        else:
            raise RuntimeError("arena full allocating %s (%d words); free=%s" % (name, nw, free_list))
        live[name] = (off, nw)
        stack.callback(a_free, name)
        base = arena[0:shape[0], off:off + nw]
        if dt != F32:
            base = base.bitcast(dt)
        base = base[:, 0:per]
        if len(shape) == 3:
            base = base.rearrange("p (a b) -> p a b", b=shape[2])
        elif len(shape) == 4:
            base = base.rearrange("p (a b c) -> p a b c", b=shape[2], c=shape[3])
        return base

    PS = [nc.alloc_psum_tensor("ps%d" % i, [128, 512], F32).ap() for i in range(8)]
    PSN = ["ps%d" % i for i in range(8)]

    vec = sb(es, "vec", [128, NV], F32)
    flg = sb(es, "flg", [128, 32], F32)
    l0 = ExitStack()
    cstA = sb(es, "cstA", [128, 128], F32)
    dma("sp", vec, vec_in, w=["vec"])
    dma("sp", flg, flags_in, w=["flg"])
    dma("sp", cstA, cst_in[:, C_ID:C_ID + 128], w=["cstA"])
    ident_f = cstA
    ident_b = sb(es, "ident_b", [128, 128], BF16)
    op(ACT, lambda: S.copy(out=ident_b, in_=ident_f), r=["cstA"], w=["ident_b"])
    od1024 = sb(es, "od1024", [128, 128], BF16)
    od128 = sb(es, "od128", [128, 128], BF16)
    op(DVE, lambda: V.memset(od1024, 1.0 / 1024.0), w=["od1024"])
    op(DVE, lambda: V.memset(od128, 1.0 / 128.0), w=["od128"])

    mT = sb(es, "mT", [128, 2 * 48 * 4], F32)
    mod = sb(es, "mod", [128, 2 * 6 * 8 * 4], F32)
    scT = sb(es, "scT", [128, 32], F32)
    lbv = sb(es, "lbv", [128, 16], F32)
    clv = sb(es, "clv", [128, 8], F32)
    tmpv = sb(es, "tmpv", [128, 64], F32)

    def MOD(l, j, k, b):
        o = ((l * 6 + j) * 8 + k) * 4 + b
        return mod[:, o:o + 1]

    op(ACT, lambda: S.activation(out=scT, in_=vec[:, V_CT:V_CT + 32], func=AF.Silu), r=["vec"], w=["scT"])
    with ExitStack() as ph:
        wad = [sb(ph, "wad%d" % i, [128, 8, 512], F32) for i in range(2)]
        it = 0
        for l in range(2):
            wv = w_ada[l].rearrange("(k p) c -> p k c", p=128)
            for g in range(12):
                wb = wad[it % 2]; wn = "wad%d" % (it % 2)
                dma("sp", wb, wv[:, :, g * 512:(g + 1) * 512], w=[wn])
                psb = PS[it % 2]; pn = PSN[it % 2]

                def mm(wb=wb, psb=psb):
                    ins = None
                    for cl in range(4):
                        for k in range(8):
                            ins = TE.matmul(psb[:, cl * 4:cl * 4 + 4], lhsT=wb[:, k, cl * 128:(cl + 1) * 128],
                                            rhs=scT[:, k * 4:k * 4 + 4], start=(k == 0), stop=(k == 7))
                    return ins
                op(PE, mm, r=[wn, "scT"], w=[pn])
                o0 = (l * 48 + g * 4) * 4
                op(DVE, lambda psb=psb, o0=o0: V.tensor_copy(out=mT[:, o0:o0 + 16], in_=psb[:, 0:16]), r=[pn], w=["mT"])
                it += 1
        T.barrier()
    for l in range(2):
        mv = mT[:, l * 192:(l + 1) * 192].rearrange("p (c b) -> p c b", b=4)
        bv = vec[:, V_BADA + l * 48:V_BADA + (l + 1) * 48].unsqueeze(2).to_broadcast([128, 48, 4])
        op(DVE, lambda mv=mv, bv=bv: V.tensor_tensor(out=mv, in0=mv, in1=bv, op=ALU.add), r=["vec", "mT"], w=["mT"])

        def m4(i, l=l):
            return mT[:, l * 192 + i * 32:l * 192 + (i + 1) * 32]

        def mo(j, l=l):
            return mod[:, (l * 6 + j) * 32:(l * 6 + j + 1) * 32]
        nmx = vec[:, V_NMIX + l * 8:V_NMIX + l * 8 + 8].unsqueeze(2).to_broadcast([128, 8, 4])
        nff = vec[:, V_NFFN + l * 8:V_NFFN + l * 8 + 8].unsqueeze(2).to_broadcast([128, 8, 4])
        v3 = lambda a: a.rearrange("p (k b) -> p k b", b=4)
        op(DVE, lambda: V.scalar_tensor_tensor(out=v3(mo(0)), in0=v3(m4(1)), scalar=1.0, in1=nmx, op0=ALU.add, op1=ALU.mult), r=["mT", "vec"], w=["mod"])
        op(DVE, lambda: V.tensor_copy(out=mo(1), in_=m4(0)), r=["mT"], w=["mod"])
        op(DVE, lambda: V.tensor_scalar(out=mo(2), in0=m4(2), scalar1=1.0, scalar2=None, op0=ALU.add), r=["mT"], w=["mod"])
        op(DVE, lambda: V.scalar_tensor_tensor(out=v3(mo(3)), in0=v3(m4(4)), scalar=1.0, in1=nff, op0=ALU.add, op1=ALU.mult), r=["mT", "vec"], w=["mod"])
        op(DVE, lambda: V.tensor_copy(out=mo(4), in_=m4(3)), r=["mT"], w=["mod"])
        op(DVE, lambda: V.tensor_scalar(out=mo(5), in0=m4(5), scalar1=1.0, scalar2=None, op0=ALU.add), r=["mT"], w=["mod"])
    op(ACT, lambda: S.activation(out=tmpv[:, 0:12], in_=vec[:, V_LB:V_LB + 12], func=AF.Exp), r=["vec"], w=["tmpv"])
    op(DVE, lambda: V.tensor_tensor(out=tmpv[:, 12:16], in0=tmpv[:, 0:4], in1=tmpv[:, 4:8], op=ALU.add), r=["tmpv"], w=["tmpv"])
    op(DVE, lambda: V.tensor_tensor(out=tmpv[:, 12:16], in0=tmpv[:, 12:16], in1=tmpv[:, 8:12], op=ALU.add), r=["tmpv"], w=["tmpv"])
    op(DVE, lambda: V.reciprocal(out=tmpv[:, 16:20], in_=tmpv[:, 12:16]), r=["tmpv"], w=["tmpv"])
    op(DVE, lambda: V.tensor_tensor(out=lbv[:, 0:4], in0=tmpv[:, 0:4], in1=tmpv[:, 16:20], op=ALU.mult), r=["tmpv"], w=["lbv"])
    op(DVE, lambda: V.tensor_scalar(out=lbv[:, 4:8], in0=lbv[:, 0:4], scalar1=-1.0, scalar2=1.0, op0=ALU.mult, op1=ALU.add), r=["lbv"], w=["lbv"])
    op(DVE, lambda: V.tensor_scalar(out=lbv[:, 8:12], in0=lbv[:, 0:4], scalar1=1.0, scalar2=-1.0, op0=ALU.mult, op1=ALU.add), r=["lbv"], w=["lbv"])
    op(ACT, lambda: S.activation(out=tmpv[:, 32:40], in_=vec[:, V_LAM:V_LAM + 8], func=AF.Exp, scale=-1.0), r=["vec"], w=["tmpv"])
    op(ACT, lambda: S.activation(out=tmpv[:, 40:48], in_=tmpv[:, 32:40], func=AF.Ln, bias=1.0), r=["tmpv"], w=["tmpv"])
    op(DVE, lambda: V.tensor_scalar(out=clv, in0=tmpv[:, 40:48], scalar1=-8.0, scalar2=None, op0=ALU.mult), r=["tmpv"], w=["clv"])

    wk = es
    sq = [sb(wk, "sq%d" % i, [128, 512], BF16) for i in range(2)]
    rt = sb(wk, "rt", [128, 512], F32)
    rstd = sb(wk, "rstd", [128, 512], F32)
    ntmp = [sb(wk, "ntmp%d" % i, [128, 512], F32) for i in range(2)]

    def norm_tile(src, srcn, N, parts, dst, dstn, psi, jg, jsh):
        def stat():
            ins = None
            return ins
        for k in range(8):
            q = sq[k % 2]; qn = "sq%d" % (k % 2)
            op(ACT, lambda k=k, q=q: S.activation(out=q[:, :N], in_=src[:, k, :N], func=AF.Square), r=[srcn], w=[qn])
            op(PE, lambda k=k, q=q: TE.matmul(PS[psi][:, :N], lhsT=od1024, rhs=q[:, :N], start=(k == 0), stop=(k == 7)),
               r=[qn, "od1024"], w=[PSN[psi]])
        op(ACT, lambda: S.activation(out=rt[:, :N], in_=PS[psi][:, :N], func=AF.Sqrt, bias=EPS), r=[PSN[psi]], w=["rt"])
        op(DVE, lambda: V.reciprocal(out=rstd[:, :N], in_=rt[:, :N]), r=["rt"], w=["rstd"])
        for k in range(8):
            tb = ntmp[k % 2]; tn = "ntmp%d" % (k % 2)
            op(DVE, lambda k=k, tb=tb: V.tensor_tensor(out=tb[:, :N], in0=src[:, k, :N], in1=rstd[:, :N], op=ALU.mult), r=[srcn, "rstd"], w=[tn])
            for (c0, c1, l, b) in parts:
                op(ACT, lambda k=k, tb=tb, c0=c0, c1=c1, l=l, b=b: S.activation(
                    out=dst[:, k, c0:c1], in_=tb[:, c0:c1], func=AF.Identity, scale=MOD(l, jg, k, b), bias=MOD(l, jsh, k, b)),
                    r=[tn, "mod"], w=[dstn])

    xin = [None, None]
    xin_ctr = [0]

    def load_xT(src_rows, nsub, dst, dstn, col0, psA, psB):
        for s in range(nsub):
            i = xin_ctr[0] % 2; xin_ctr[0] += 1
            xb = xin[i]; xn = "xin%d" % i
            dma("sp", xb, src_rows[s * 128:(s + 1) * 128, :], w=[xn])
            for half in range(2):
                pi = psA if half == 0 else psB

                def tr(xb=xb, half=half, pi=pi):
                    ins = None
                    for kk in range(4):
                        k = half * 4 + kk
                        ins = TE.transpose(PS[pi][:, kk * 128:(kk + 1) * 128], xb[:, k * 128:(k + 1) * 128], ident_f)
                    return ins
                op(PE, tr, r=[xn, "cstA"], w=[PSN[pi]])
                eng = ACT if half == 0 else DVE
                dv = dst[:, half * 4:half * 4 + 4, col0 + s * 128:col0 + (s + 1) * 128]
                pv = PS[pi][:, :].rearrange("p (k t) -> p k t", t=128)
                if eng == ACT:
                    op(ACT, lambda dv=dv, pv=pv: S.copy(out=dv, in_=pv), r=[PSN[pi]], w=[dstn])
                else:
                    op(DVE, lambda dv=dv, pv=pv: V.tensor_copy(out=dv, in_=pv), r=[PSN[pi]], w=[dstn])

    Sst = sb(es, "Sst", [128, 3, 4, 128], F32)
    Sbf = sb(es, "Sbf", [128, 3, 4, 128], BF16)
    hist = sb(es, "hist", [128, 3, 24], F32)
    carry = sb(es, "carry", [128, 3, 8], F32)
    dbg = {}
    FN = {}
    if os.environ.get("KDBG"):
        dbg["x"] = nc.dram_tensor("dbg_x", [128, 8 * (TOWN + NSMP)], F32, kind="ExternalOutput").ap()

    NSLOT0 = int(os.environ.get("KSLOT0", "0"))

    def run_slot(j, is_own):
        ls = ExitStack()
        cst = sb(ls, "cst", [128, NCST], F32)
        dma("sp", cst, cst_in, w=["cst"])
        cmask = cst[0:64, C_CM:C_CM + 64]
        rsmask = cst[:, C_RS:C_RS + 512]
        xin[0] = sb(ls, "xin0", [128, D], F32); xin[1] = sb(ls, "xin1", [128, D], F32)
        xsrc_rows = x_own if is_own else x_pre[j * TOWN:(j + 1) * TOWN, :]
        obT = sb(ls, "obT", [128, 4, TOWN + NSMP], BF16)
        kTs = sb(ls, "kTs", [128, 4, NSMP], BF16)
        vs_new = sb(ls, "vs_new", [64, 2, 512], BF16)
        qTs = sb(ls, "qTs", [128, 4, NSMP], BF16)

        with ExitStack() as ph:
            wie = sb(ph, "wie", [128, 8, 3584], BF16)
            wv = w_in_even.rearrange("(k p) c -> p k c", p=128)
            wie_s3 = wie_s.rearrange("p (k c) -> p k c", c=3584)
            for g in range(7):
                if "wie" not in WCACHED:
                    dma("pool", wie[:, :, g * 512:(g + 1) * 512], wv[:, :, g * 512:(g + 1) * 512], w=["wie%d" % g])
                    dma("sp", wie_s3[:, :, g * 512:(g + 1) * 512], wie[:, :, g * 512:(g + 1) * 512], r=["wie%d" % g], w=["wie_s"])
                else:
                    dma("sp", wie[:, :, g * 512:(g + 1) * 512], wie_s3[:, :, g * 512:(g + 1) * 512], r=["wie_s"], w=["wie%d" % g])
            WCACHED.add("wie")
            WIE = ["wie%d" % g for g in range(7)]
            xTt = sb(ph, "xTt", [128, 8, 512], F32)
            hnT = [sb(ph, "hnT%d" % i, [128, 8, 512], BF16) for i in range(2)]
            stg_b = [sb(ph, "stgb%d" % i, [128, 4, 512], BF16) for i in range(2)]
            stg_f = [sb(ph, "stgf%d" % i, [128, 512], F32) for i in range(2)]
            stg_v = [sb(ph, "stgv%d" % i, [128, 512], BF16) for i in range(2)]
            ibt = sb(ph, "ibt", [64, 8, 512], BF16)
            hb = {}
            for nm in ("sg", "ff", "kk", "lf", "bb", "d3", "qs"):
                hb[nm] = [sb(ph, "h_%s0" % nm, [128, 512], F32)] * 2
            hb["d1"] = hb["d3"]
            for nm in ("qt", "qh", "kt", "kh"):
                hb[nm] = [sb(ph, "h_%s%d" % (nm, i), [128, 512], BF16) for i in range(2)]
            gsil = sb(ph, "gsil", [128, 4, 512], BF16)
            khat = [sb(ph, "khat%d" % i, [64, 8, 128], BF16) for i in range(2)]
            attT = [sb(ph, "attT%d" % i, [64, 8, 64], BF16) for i in range(2)]
            ebl = [sb(ph, "ebl%d" % i, [128, 8], F32) for i in range(2)]
            osq, ort, ors, otm = sq[0], rt, rstd, ntmp[0]
            stg_ctr = [0, 0, 0]

            if j == NSLOT0:
                op(DVE, lambda: V.memset(Sst[:, 0], 0.0), w=["S0"])
                op(POOL, lambda: G.memset(Sbf[:, 0], 0.0), w=["Sb0"])
            for s in range(2 if is_own else 0):
                dma("sp", Sst[:, 1 + s], st_hgrn[s].rearrange("h c v -> c h v"), w=["S%d" % (1 + s)])
                op(ACT, lambda s=s: S.copy(out=Sbf[:, 1 + s], in_=Sst[:, 1 + s]), r=["S%d" % (1 + s)], w=["Sb%d" % (1 + s)])

            def proj_fm(col0, hn, hnn, N, psi):
                def f():
                    ins = None
                    for k in range(8):
                        ins = TE.matmul(PS[psi][:, :N], lhsT=wie[:, k, col0:col0 + 128], rhs=hn[:, k, :N], start=(k == 0), stop=(k == 7))
                    return ins
                op(PE, f, r=[hnn, WIE[col0 // 512]], w=[PSN[psi]])

            def proj_tm(col0, hn, hnn, t0, M, psi):
                def f():
                    ins = None
                    for k in range(8):
                        ins = TE.matmul(PS[psi][:M, :512], lhsT=hn[:, k, t0:t0 + M], rhs=wie[:, k, col0:col0 + 512], start=(k == 0), stop=(k == 7))
                    return ins
                op(PE, f, r=[hnn, WIE[col0 // 512]], w=[PSN[psi]])

            def l0_tile(kind, src_rows, N, parts, key0, own0, tidx, chunk_state, flag_after):
                if kind not in KINDS:
                    return
                nsub = N // 128
                nch = N // 64
                hn = hnT[tidx % 2]; hnn = "hnT%d" % (tidx % 2)
                load_xT(src_rows, nsub, xTt, "xTt", 0, 0, 1)
                if SUB < 1:
                    return
                norm_tile(xTt, "xTt", N, parts, hn, hnn, 0, 0, 1)
                if SUB < 2:
                    return
                pj = [0]

                def nextps():
                    pj[0] += 1
                    return pj[0] % 2
                if kind in ("own", "smp"):
                    sbq = stg_b[stg_ctr[0] % 2]; sbqn = "stgb%d" % (stg_ctr[0] % 2); stg_ctr[0] += 1
                    for c4 in range(4):
                        pi = nextps()
                        proj_fm(c4 * 128, hn, hnn, N, pi)
                        dstq = sbq[:, c4, :N] if kind == "own" else qTs[:, c4, :N]
                        dn = sbqn if kind == "own" else "qTs"
                        op(ACT, lambda dstq=dstq, pi=pi: S.activation(out=dstq, in_=PS[pi][:, :N], func=AF.Copy, scale=0.125), r=[PSN[pi]], w=[dn])
                    if kind == "own":
                        dma("sp", qT_scr[:, :, own0:own0 + N].rearrange("h p t -> p h t"), sbq[:, :, :N], r=[sbqn], w=["qT_scr"])
                sbk = stg_b[stg_ctr[0] % 2]; sbkn = "stgb%d" % (stg_ctr[0] % 2); stg_ctr[0] += 1
                for c4 in range(4):
                    pi = nextps()
                    proj_fm(512 + c4 * 128, hn, hnn, N, pi)
                    dstk = sbk[:, c4, :N] if kind != "smp" else kTs[:, c4, :N]
                    dn = sbkn if kind != "smp" else "kTs"
                    op(DVE, lambda dstk=dstk, pi=pi: V.tensor_copy(out=dstk, in_=PS[pi][:, :N]), r=[PSN[pi]], w=[dn])
                if kind != "smp":
                    dma("sp", kT_scr[:, :, key0:key0 + N].rearrange("h p t -> p h t"), sbk[:, :, :N], r=[sbkn], w=["kT_scr"])
                if kind in ("own", "smp"):
                    for s in range(nsub):
                        pi = nextps()
                        proj_tm(512, hn, hnn, s * 128, 128, pi)
                        sf = stg_f[stg_ctr[1] % 2]; sfn = "stgf%d" % (stg_ctr[1] % 2); stg_ctr[1] += 1
                        op(ACT, lambda sf=sf, pi=pi: S.copy(out=sf, in_=PS[pi][:, :]), r=[PSN[pi]], w=[sfn])
                        dst = k_own[own0 + s * 128:own0 + (s + 1) * 128, :] if kind == "own" else k_smp[:, :]
                        if is_own:
                            dma("sp", dst, sf, r=[sfn], w=["k_out"])
                if kind != "smp":
                    for s in range(nsub):
                        pi = nextps()
                        proj_tm(1024, hn, hnn, s * 128, 128, pi)
                        if kind == "own":
                            sf = stg_f[stg_ctr[1] % 2]; sfn = "stgf%d" % (stg_ctr[1] % 2); stg_ctr[1] += 1
                            op(ACT, lambda sf=sf, pi=pi: S.copy(out=sf, in_=PS[pi][:, :]), r=[PSN[pi]], w=[sfn])
                            if is_own:
                                dma("sp", v_own[own0 + s * 128:own0 + (s + 1) * 128, :], sf, r=[sfn], w=["v_out"])
                        sv = stg_v[stg_ctr[2] % 2]; svn = "stgv%d" % (stg_ctr[2] % 2); stg_ctr[2] += 1
                        op(DVE, lambda sv=sv, pi=pi: V.tensor_copy(out=sv, in_=PS[pi][:, :]), r=[PSN[pi]], w=[svn])
                        kt = (key0 + s * 128) // 128
                        dma("sp", v_scr[:, :, kt * 128:(kt + 1) * 128].rearrange("h p c -> p h c"),
                            sv[:, :].rearrange("p (h c) -> p h c", c=128), r=[svn], w=["v_scr"])
                else:
                    for s in range(2):
                        pi = nextps()
                        proj_tm(1024, hn, hnn, s * 64, 64, pi)
                        sf = stg_f[stg_ctr[1] % 2]; sfn = "stgf%d" % (stg_ctr[1] % 2); stg_ctr[1] += 1
                        op(ACT, lambda sf=sf, pi=pi: S.copy(out=sf[0:64, :], in_=PS[pi][0:64, :]), r=[PSN[pi]], w=[sfn])
                        dma("sp", v_smp[s * 64:(s + 1) * 64, :], sf[0:64, :], r=[sfn], w=["v_out"])
                        op(DVE, lambda s=s, pi=pi: V.tensor_copy(out=vs_new[:, s, :], in_=PS[pi][0:64, :]), r=[PSN[pi]], w=["vs_new"])
                if SUB < 3:
                    return
                for j in range(nch):
                    pi = nextps()
                    proj_tm(2560, hn, hnn, j * 64, 64, pi)
                    eng = ACT if j % 2 == 0 else DVE
                    if eng == ACT:
                        op(ACT, lambda j=j, pi=pi: S.copy(out=ibt[:, j, :], in_=PS[pi][0:64, :]), r=[PSN[pi]], w=["ibt"])
                    else:
                        op(DVE, lambda j=j, pi=pi: V.tensor_copy(out=ibt[:, j, :], in_=PS[pi][0:64, :]), r=[PSN[pi]], w=["ibt"])
                full = kind != "pre"
                rec_lists = []
                for h in range(4):
                    cur = []
                    split = [None]
                    OP[0] = (lambda eng, fn, r=(), w=(), cur=cur: cur.append((eng, fn, r, w)))
                    i2 = h % 2
                    B = {nm: hb[nm][i2] for nm in hb}
                    Bn = {nm: "h_%s%d" % (nm, i2 if nm in ("qt", "qh", "kt", "kh") else 0) for nm in hb}
                    Bn["d1"] = Bn["d3"]
                    pi = nextps()
                    proj_fm(2048 + h * 128, hn, hnn, N, pi)
                    op(ACT, lambda pi=pi, B=B: S.activation(out=B["sg"][:, :N], in_=PS[pi][:, :N], func=AF.Sigmoid), r=[PSN[pi]], w=[Bn["sg"]])
                    op(DVE, lambda B=B, h=h: V.tensor_scalar(out=B["ff"][:, :N], in0=B["sg"][:, :N], scalar1=lbv[:, 4 + h:5 + h], scalar2=lbv[:, h:h + 1],
                                                             op0=ALU.mult, op1=ALU.add), r=[Bn["sg"], "lbv"], w=[Bn["ff"]])
                    op(POOL, lambda B=B, h=h: G.tensor_scalar(out=B["kk"][:, :N], in0=B["sg"][:, :N], scalar1=lbv[:, 8 + h:9 + h], scalar2=lbv[:, 4 + h:5 + h],
                                                              op0=ALU.mult, op1=ALU.add), r=[Bn["sg"], "lbv"], w=[Bn["kk"]])
                    op(ACT, lambda B=B: S.activation(out=B["lf"][:, :N], in_=B["ff"][:, :N], func=AF.Ln), r=[Bn["ff"]], w=[Bn["lf"]])
                    op(DVE, lambda B=B: V.tensor_tensor_scan(out=B["bb"][:, :N], data0=rsmask[:, :N], data1=B["lf"][:, :N], initial=0.0,
                                                             op0=ALU.mult, op1=ALU.add), r=[Bn["lf"], "cst"], w=[Bn["bb"]])
                    b3 = B["bb"][:, :N].rearrange("p (j t) -> p j t", t=64)
                    bL = b3[:, :, 63:64]
                    bM = b3[:, :, 31:32]
                    eb = ebl[i2]; ebn = "ebl%d" % i2
                    op(ACT, lambda eb=eb, bL=bL: S.activation(out=eb[:, :nch].unsqueeze(2), in_=bL, func=AF.Exp), r=[Bn["bb"]], w=[ebn])
                    d3v = B["d3"][:, :N].rearrange("p (j t) -> p j t", t=64)
                    op(DVE, lambda d3v=d3v, bL=bL, b3=b3: V.tensor_tensor(out=d3v, in0=bL.to_broadcast([128, nch, 64]), in1=b3, op=ALU.subtract),
                       r=[Bn["bb"]], w=[Bn["d3"]])
                    op(ACT, lambda B=B: S.activation(out=B["d3"][:, :N], in_=B["d3"][:, :N], func=AF.Exp), r=[Bn["d3"]], w=[Bn["d3"]])
                    op(POOL, lambda B=B: G.tensor_tensor(out=B["kh"][:, :N], in0=B["kk"][:, :N], in1=B["d3"][:, :N], op=ALU.mult),
                       r=[Bn["kk"], Bn["d3"]], w=[Bn["kh"]])
                    pt = 2 + i2
                    ptv = PS[pt][:, :].bitcast(BF16)

                    def trk(B=B, ptv=ptv):
                        ins = None
                        for j in range(nch):
                            ins = TE.transpose(ptv[0:64, j * 128:(j + 1) * 128], B["kh"][:, j * 64:(j + 1) * 64], ident_b)
                        return ins
                    op(PE, trk, r=[Bn["kh"], "ident_b"], w=[PSN[pt]])
                    kh = khat[i2]; khn = "khat%d" % i2
                    op(ACT, lambda kh=kh, ptv=ptv: S.copy(out=kh[:, :nch, :], in_=ptv[0:64, :nch * 128].rearrange("p (j c) -> p j c", c=128)),
                       r=[PSN[pt]], w=[khn])
                    if full:
                        d1v = B["d1"][:, :N].rearrange("p (j t) -> p j t", t=64)
                        op(DVE, lambda d1v=d1v, bM=bM, b3=b3: V.tensor_tensor(out=d1v, in0=b3, in1=bM.to_broadcast([128, nch, 64]), op=ALU.subtract),
                           r=[Bn["bb"]], w=[Bn["d1"]])
                        pq = nextps()
                        proj_fm(1536 + h * 128, hn, hnn, N, pq)
                        op(ACT, lambda pq=pq, B=B: S.activation(out=B["qs"][:, :N], in_=PS[pq][:, :N], func=AF.Silu), r=[PSN[pq]], w=[Bn["qs"]])
                        op(ACT, lambda B=B: S.activation(out=B["sg"][:, :N], in_=B["d1"][:, :N], func=AF.Exp), r=[Bn["d1"]], w=[Bn["sg"]])
                        op(DVE, lambda B=B: V.scalar_tensor_tensor(out=B["qt"][:, :N], in0=B["qs"][:, :N], scalar=128.0 ** -0.5, in1=B["sg"][:, :N],
                                                                   op0=ALU.mult, op1=ALU.mult), r=[Bn["qs"], Bn["sg"]], w=[Bn["qt"]])
                        op(ACT, lambda B=B: S.activation(out=B["ff"][:, :N], in_=B["d1"][:, :N], func=AF.Exp, scale=-1.0), r=[Bn["d1"]], w=[Bn["ff"]])
                        op(POOL, lambda B=B: G.tensor_tensor(out=B["kt"][:, :N], in0=B["kk"][:, :N], in1=B["ff"][:, :N], op=ALU.mult),
                           r=[Bn["kk"], Bn["ff"]], w=[Bn["kt"]])
                        op(ACT, lambda B=B: S.activation(out=B["lf"][:, :N], in_=B["bb"][:, :N], func=AF.Exp), r=[Bn["bb"]], w=[Bn["lf"]])
                        op(DVE, lambda B=B: V.scalar_tensor_tensor(out=B["qh"][:, :N], in0=B["qs"][:, :N], scalar=128.0 ** -0.5, in1=B["lf"][:, :N],
                                                                   op0=ALU.mult, op1=ALU.mult), r=[Bn["qs"], Bn["lf"]], w=[Bn["qh"]])
                        pg = nextps()
                        proj_fm(3072 + h * 128, hn, hnn, N, pg)
                        op(ACT, lambda pg=pg, h=h: S.activation(out=gsil[:, h, :N], in_=PS[pg][:, :N], func=AF.Silu), r=[PSN[pg]], w=["gsil"])
                        pa = 4 + i2

                        def att(B=B, pa=pa):
                            ins = None
                            for j in range(nch):
                                ins = TE.matmul(PS[pa][0:64, j * 64:(j + 1) * 64], lhsT=B["kt"][:, j * 64:(j + 1) * 64],
                                                rhs=B["qt"][:, j * 64:(j + 1) * 64], start=True, stop=True)
                            return ins
                        op(PE, att, r=[Bn["kt"], Bn["qt"]], w=[PSN[pa]])
                        at = attT[i2]; atn = "attT%d" % i2
                        op(DVE, lambda at=at, pa=pa: V.tensor_tensor(out=at[:, :nch, :], in0=PS[pa][0:64, :nch * 64].rearrange("p (j t) -> p j t", t=64),
                                                                     in1=cmask.unsqueeze(1).to_broadcast([64, nch, 64]), op=ALU.mult),
                           r=[PSN[pa], "cst"], w=[atn])
                    split[0] = len(cur)
                    po = 6 + i2
                    for j in range(nch):
                        si = chunk_state(j)
                        Sn, Sbn = "S%d_%d" % (si, h), "Sb%d_%d" % (si, h)
                        base_r = ["S%d" % si, "Sb%d" % si]
                        if full:
                            def omm(j=j, si=si, h=h, at=at, B=B, po=po):
                                TE.matmul(PS[po][:, j * 64:(j + 1) * 64], lhsT=ibt[:, j, h * 128:(h + 1) * 128], rhs=at[:, j, :], start=True, stop=False)
                                return TE.matmul(PS[po][:, j * 64:(j + 1) * 64], lhsT=Sbf[:, si, h, :], rhs=B["qh"][:, j * 64:(j + 1) * 64], start=False, stop=True)
                            op(PE, omm, r=["ibt", atn, Sbn, "Sb%d" % si, Bn["qh"]], w=[PSN[po]])
                        pd = 2 + i2
                        op(PE, lambda j=j, kh=kh, h=h, pd=pd: TE.matmul(PS[pd][:, 256 + (j % 2) * 128:256 + (j % 2) * 128 + 128], lhsT=kh[:, j, :],
                                                                        rhs=ibt[:, j, h * 128:(h + 1) * 128], start=True, stop=True),
                           r=[khn, "ibt"], w=[PSN[pd]])
                        op(DVE, lambda j=j, si=si, h=h, eb=eb, pd=pd: V.scalar_tensor_tensor(
                            out=Sst[:, si, h, :], in0=Sst[:, si, h, :], scalar=eb[:, j:j + 1], in1=PS[pd][:, 256 + (j % 2) * 128:256 + (j % 2) * 128 + 128],
                            op0=ALU.mult, op1=ALU.add), r=[PSN[pd], ebn, "S%d" % si, Sn], w=[Sn])
                        op(ACT, lambda si=si, h=h: S.copy(out=Sbf[:, si, h, :], in_=Sst[:, si, h, :]), r=[Sn], w=[Sbn])
                    if full:
                        op(ACT, lambda po=po: S.activation(out=osq[:, :N], in_=PS[po][:, :N], func=AF.Square), r=[PSN[po]], w=["sq0"])
                        pm = nextps()
                        op(PE, lambda pm=pm: TE.matmul(PS[pm][:, :N], lhsT=od128, rhs=osq[:, :N], start=True, stop=True), r=["sq0", "od128"], w=[PSN[pm]])
                        op(ACT, lambda pm=pm: S.activation(out=ort[:, :N], in_=PS[pm][:, :N], func=AF.Sqrt, bias=EPS), r=[PSN[pm]], w=["rt"])
                        op(DVE, lambda: V.reciprocal(out=ors[:, :N], in_=ort[:, :N]), r=["rt"], w=["rstd"])
                        op(DVE, lambda po=po, h=h: V.scalar_tensor_tensor(out=otm[:, :N], in0=PS[po][:, :N], scalar=vec[:, V_GN + h:V_GN + h + 1], in1=ors[:, :N],
                                                                         op0=ALU.mult, op1=ALU.mult), r=[PSN[po], "rstd", "vec"], w=["ntmp0"])
                        oc0 = own0 if kind == "own" else TOWN
                        op(POOL, lambda h=h, oc0=oc0: G.tensor_tensor(out=obT[:, h, oc0:oc0 + N], in0=otm[:, :N], in1=gsil[:, h, :N], op=ALU.mult),
                           r=["ntmp0", "gsil"], w=["obT"])
                    rec_lists.append((cur, split[0]))
                OP[0] = T.op
                preps = [c_[:s_] for (c_, s_) in rec_lists]
                recs = [c_[s_:] for (c_, s_) in rec_lists]
                for a_ in preps[0]:
                    T.op(*a_)
                for h in range(4):
                    if h + 1 < 4:
                        for a_ in preps[h + 1]:
                            T.op(*a_)
                    for a_ in recs[h]:
                        T.op(*a_)
                if flag_after is not None:
                    for h in range(4):
                        op(DVE, lambda h=h: V.tensor_scalar(out=Sst[:, 0, h, :], in0=Sst[:, 0, h, :], scalar1=flg[:, flag_after:flag_after + 1], scalar2=None,
                                                            op0=ALU.mult), r=["S0_%d" % h, "flg"], w=["S0_%d" % h])
                        op(ACT, lambda h=h: S.copy(out=Sbf[:, 0, h, :], in_=Sst[:, 0, h, :]), r=["S0_%d" % h], w=["Sb0_%d" % h])

            tix = 0
            prm = [(0, 512, 0, 0)]
            if STAGE >= 1:
                for t in range(TOWN // 512):
                    fa = j if (t == 3 and not is_own) else None
                    l0_tile("own", xsrc_rows[t * 512:(t + 1) * 512, :], 512, prm, j * TOWN + t * 512, t * 512, tix, lambda jj: 0, fa)
                    tix += 1
                if is_own:
                    dma("sp", S_p.rearrange("h c v -> c h v"), Sst[:, 0], r=["S0_%d" % h for h in range(4)], w=["S_p"])
                    l0_tile("smp", x_smp, 128, [(0, 64, 0, 1), (64, 128, 0, 2)], 0, 0, tix, lambda jj: 1 + jj, None)
                    for s in range(2):
                        dma("sp", S_s[s].rearrange("h c v -> c h v"), Sst[:, 1 + s], r=["S%d_%d" % (1 + s, h) for h in range(4)], w=["S_s"])
            T.barrier()


        oaT = sb(ls, "oaT", [128, 4, TOWN + NSMP], BF16)
        if STAGE >= 2:
          with ExitStack() as ph:
            bufA = [sb(ph, "bufA%d" % i, [128, 8192], BF16) for i in range(2)]
            bufB = [sb(ph, "bufB%d" % i, [128, 8192], BF16) for i in range(2)]
            qTb = [sb(ph, "qTb%d" % i, [128, TOWN], BF16) for i in range(2)]
            ones512 = sb(ph, "ones512", [128, 512], F32)
            op(DVE, lambda: V.memset(ones512, 1.0), w=["ones512"])
            W_ = {}
            for nm, shp, dt in (("e", [128, 512], F32), ("sp", [128, 512], F32), ("P", [128, 520], F32), ("a", [128, 512], F32),
                                ("zs", [128, 512], F32), ("w", [128, 512], BF16), ("wT", [128, 4, 128], BF16)):
                W_[nm] = [sb(ph, "at_%s%d" % (nm, i), shp, dt) for i in range(3)]
            nbuf = [sb(ph, "nbuf%d" % i, [128, 40], F32) for i in range(5)]
            for i in range(3):
                op(DVE, lambda i=i: V.memset(W_["P"][i][:, 0:1], 0.0), w=["at_P%d" % i])
            for i in range(5):
                op(DVE, lambda i=i: V.memset(nbuf[i], 0.0), w=["nb%d" % i])
            tctr = [0]
            qctr = [0]

            pipe = []
            qo = [0]
            cpc = [0]

            ORDER = (("A0", 0), ("A1", 1), ("C1", 3), ("A2", 1), ("C3", 4), ("B1", 2), ("B2", 2), ("C2", 3))

            def push(job):
                pipe.append(job)
                n_ = len(pipe)
                for (st_, lag) in ORDER:
                    idx = n_ - 1 - lag
                    if idx >= 0:
                        pipe[idx][st_]()

            def flush():
                n_ = len(pipe)
                for k_ in range(1, 5):
                    for (st_, lag) in ORDER:
                        idx = n_ - 1 + k_ - lag
                        if 0 <= idx < n_:
                            pipe[idx][st_]()
                del pipe[:]

            def att_qtile(M, qT_ap, qn, tiles, pb, dst_ap, dstn):
                qi = qctr[0] % 5; qctr[0] += 1
                ob = 6 + qo[0] % 2; qo[0] += 1
                nb = nbuf[qi]
                nt = len(tiles)
                for ti, tl in enumerate(tiles):
                    sl = tctr[0] % 3
                    tb = 4 + tctr[0] % 2
                    zb = tctr[0] % 4
                    tctr[0] += 1
                    n = tl["n"]
                    B = {k_: W_[k_][sl] for k_ in W_}
                    Bn = {k_: "at_%s%d" % (k_, sl) for k_ in W_}
                    fb = tl["fb"]
                    if tl["mask"] is not None:
                        src, srcn = B["zs"][:M, :n], Bn["zs"]
                    else:
                        src, srcn = PS[zb][:M, :n], PSN[zb]
                    c0, c1 = "nb%d_%d" % (qi, ti), "nb%d_%d" % (qi, ti + 1)
                    bias_ap, bn_ = nb[:M, ti + 1:ti + 2], c1
                    nsub = (n + 127) // 128
                    ks = min(n, 128)
                    tv = PS[tb][:, :].bitcast(BF16)

                    def stA0(tl=tl, zb=zb, n=n):
                        op(PE, lambda: TE.matmul(PS[zb][:M, :n], lhsT=qT_ap, rhs=tl["kT"], start=True, stop=True), r=[qn, tl["kn"]], w=[PSN[zb]])

                    def stA(tl=tl, zb=zb, n=n, B=B, Bn=Bn, src=src, srcn=srcn, fb=fb):
                        if tl["mask"] is not None:
                            op(DVE, lambda: V.tensor_tensor(out=B["zs"][:M, :n], in0=PS[zb][:M, :n], in1=tl["mask"], op=ALU.add),
                               r=[PSN[zb], "cst"], w=[Bn["zs"]])
                        if fb is not None:
                            op(ACT, lambda: S.activation(out=B["e"][:M, :n], in_=src, func=AF.Exp, bias=fb), r=[srcn, "flg"], w=[Bn["e"]])
                        else:
                            op(ACT, lambda: S.activation(out=B["e"][:M, :n], in_=src, func=AF.Exp), r=[srcn], w=[Bn["e"]])

                    def stA2(n=n, B=B, Bn=Bn):
                        op(ACT, lambda: S.activation(out=B["sp"][:M, :n], in_=B["e"][:M, :n], func=AF.Ln, bias=1.0), r=[Bn["e"]], w=[Bn["sp"]])

                    def stB(n=n, B=B, Bn=Bn, src=src, srcn=srcn, fb=fb, ti=ti, c0=c0, c1=c1, bias_ap=bias_ap, bn_=bn_):
                        op(DVE, lambda: V.tensor_tensor_scan(out=B["P"][:M, 1:n + 1], data0=ones512[:M, :n], data1=B["sp"][:M, :n], initial=0.0,
                                                             op0=ALU.mult, op1=ALU.add), r=[Bn["sp"], "ones512"], w=[Bn["P"]])

                    def stB2(n=n, B=B, Bn=Bn, src=src, srcn=srcn, fb=fb, ti=ti, c0=c0, c1=c1, bias_ap=bias_ap, bn_=bn_):
                        if fb is not None:
                            op(DVE, lambda: V.scalar_tensor_tensor(out=B["a"][:M, :n], in0=src, scalar=fb, in1=B["P"][:M, 0:n], op0=ALU.add, op1=ALU.add),
                               r=[srcn, Bn["P"], "flg"], w=[Bn["a"]])
                        else:
                            op(DVE, lambda: V.tensor_tensor(out=B["a"][:M, :n], in0=src, in1=B["P"][:M, 0:n], op=ALU.add), r=[srcn, Bn["P"]], w=[Bn["a"]])
                        op(POOL, lambda: G.tensor_tensor(out=nb[:M, ti + 1:ti + 2], in0=nb[:M, ti:ti + 1], in1=B["P"][:M, n:n + 1], op=ALU.subtract),
                           r=[Bn["P"], c0, "nb%d" % qi], w=[c1])

                    def stC(tl=tl, n=n, B=B, Bn=Bn, ti=ti, bias_ap=bias_ap, bn_=bn_, nsub=nsub, ks=ks, tv=tv, tb=tb, sl=sl):
                        op(ACT, lambda: S.activation(out=B["w"][:M, :n], in_=B["a"][:M, :n], func=AF.Exp, bias=bias_ap), r=[Bn["a"], bn_], w=[Bn["w"]])

                    def stC2(tl=tl, n=n, B=B, Bn=Bn, ti=ti, nsub=nsub, ks=ks, tv=tv, tb=tb, sl=sl):
                        def trw():
                            ins = None
                            for s_ in range(nsub):
                                ins = TE.transpose(tv[:ks, s_ * 128:s_ * 128 + M], B["w"][:M, s_ * 128:s_ * 128 + ks], ident_b[:M, :M])
                            return ins
                        op(PE, trw, r=[Bn["w"], "ident_b"], w=[PSN[tb]])

                    def stC3(tl=tl, n=n, B=B, Bn=Bn, ti=ti, nsub=nsub, ks=ks, tv=tv, tb=tb, sl=sl):
                        wTv = B["wT"][:ks, :nsub, :M]
                        tvv = tv[:ks, :nsub * 128].rearrange("p (s q) -> p s q", q=128)[:, :, :M]
                        if cpc[0] % 4 == 3:
                            op(DVE, lambda: V.tensor_copy(out=wTv, in_=tvv), r=[PSN[tb]], w=[Bn["wT"]])
                        else:
                            op(ACT, lambda: S.copy(out=wTv, in_=tvv), r=[PSN[tb]], w=[Bn["wT"]])
                        cpc[0] += 1

                        def wv():
                            ins = None
                            for s_ in range(nsub):
                                ins = TE.matmul(PS[ob][pb:pb + 64, :M], lhsT=tl["vsub"](s_), rhs=B["wT"][:ks, s_, :M],
                                                start=(ti == 0 and s_ == 0), stop=(ti == nt - 1 and s_ == nsub - 1))
                            return ins
                        op(PE, wv, r=[Bn["wT"], tl["vn"]], w=[PSN[ob]])
                        if ti == nt - 1:
                            op(ACT, lambda: S.copy(out=dst_ap, in_=PS[ob][pb:pb + 64, :M]), r=[PSN[ob]], w=[dstn])
                    push(dict(A0=stA0, A1=stA, A2=stA2, B1=stB, B2=stB2, C1=stC, C2=stC2, C3=stC3))

            NQT = int(os.environ.get("KNQT", "16"))
            NHP = int(os.environ.get("KNHP", "4"))
            for hp in range(NHP):
                i = hp % 2
                kTb, vbb, qb = bufA[i], bufB[i], qTb[i]
                nk = (j + 1) * TOWN
                dma("sp", kTb[:, 0:nk], kT_scr[hp][:, 0:nk], r=["kT_scr"], w=["bufA%d" % i])
                dma("sp", vbb[:, 0:nk], v_scr[hp][:, 0:nk], r=["v_scr"], w=["bufB%d" % i])
                dma("sp", qb, qT_scr[hp], r=["qT_scr"], w=["qTb%d" % i])
                vb3 = vbb.rearrange("p (t c) -> p t c", c=128)
                for hh in range(2):
                    pb = hh * 64
                    for qt in range(NQT):
                        tiles = []
                        for kt in range(qt // 4, -1, -1):
                            key0 = j * TOWN + kt * 512
                            diag = (kt == qt // 4)
                            m_ = qt % 4
                            tiles.append(dict(kT=kTb[pb:pb + 64, key0:key0 + 512], kn="bufA%d" % i, n=512,
                                              vsub=(lambda s_, key0=key0: vb3[:, key0 // 128 + s_, pb:pb + 64]), vn="bufB%d" % i,
                                              mask=(cst[:, C_MASK + m_ * 512:C_MASK + (m_ + 1) * 512] if diag else None), fb=None))
                        for kt in range(4 * j - 1, 4 * NSLOT0 - 1, -1):
                            key0 = kt * 512
                            jj = kt // 4
                            tiles.append(dict(kT=kTb[pb:pb + 64, key0:key0 + 512], kn="bufA%d" % i, n=512,
                                              vsub=(lambda s_, key0=key0: vb3[:, key0 // 128 + s_, pb:pb + 64]), vn="bufB%d" % i,
                                              mask=None, fb=flg[:, 3 + jj:4 + jj]))
                        att_qtile(128, qb[pb:pb + 64, qt * 128:(qt + 1) * 128], "qTb%d" % i, tiles, pb,
                                  oaT[pb:pb + 64, hp, qt * 128:(qt + 1) * 128], "oaT")
            flush()
            for s_ in range(2 if (NHP == 4 and is_own) else 0):
                for hp in range(4):
                    ckb = bufA[0][:, 0:4096].rearrange("p (t c) -> p t c", c=128)
                    kTsb = bufA[1][:, 0:4096]
                    vsb = bufB[0][:, 0:4096].rearrange("p (t c) -> p t c", c=128)
                    dma("pool", ckb, cache_k[s_].rearrange("(t p) c -> p t c", p=128)[:, :, hp * 128:(hp + 1) * 128], w=["bufA0"])
                    dma("pool", vsb, cache_v[s_].rearrange("(t p) c -> p t c", p=128)[:, :, hp * 128:(hp + 1) * 128], w=["bufB0"])
                    for rnd in range(4):
                        tb = 2 + rnd % 2
                        tv = PS[tb][:, :].bitcast(BF16)

                        def trc(rnd=rnd, tv=tv, ckb=ckb):
                            ins = None
                            for t8 in range(8):
                                ins = TE.transpose(tv[:, t8 * 128:(t8 + 1) * 128], ckb[:, rnd * 8 + t8, :], ident_b)
                            return ins
                        op(PE, trc, r=["bufA0", "ident_b"], w=[PSN[tb]])
                        if rnd % 2 == 0:
                            op(DVE, lambda rnd=rnd, tv=tv, kTsb=kTsb: V.tensor_copy(out=kTsb[:, rnd * 1024:(rnd + 1) * 1024], in_=tv), r=[PSN[tb]], w=["bufA1"])
                        else:
                            op(ACT, lambda rnd=rnd, tv=tv, kTsb=kTsb: S.copy(out=kTsb[:, rnd * 1024:(rnd + 1) * 1024], in_=tv), r=[PSN[tb]], w=["bufA1"])
                    for hh in range(2):
                        pb = hh * 64
                        h = 2 * hp + hh
                        tiles = [dict(kT=kTs[pb:pb + 64, hp, s_ * 64:(s_ + 1) * 64], kn="kTs", n=64,
                                      vsub=(lambda s2, h=h: vs_new[0:64, s_, h * 64:(h + 1) * 64]), vn="vs_new",
                                      mask=cst[0:64, C_MASK:C_MASK + 64], fb=None)]
                        for kt in range(7, -1, -1):
                            tiles.append(dict(kT=kTsb[pb:pb + 64, kt * 512:(kt + 1) * 512], kn="bufA1", n=512,
                                              vsub=(lambda s2, kt=kt, pb=pb: vsb[:, kt * 4 + s2, pb:pb + 64]), vn="bufB0", mask=None, fb=None))
                        att_qtile(64, qTs[pb:pb + 64, hp, s_ * 64:(s_ + 1) * 64], "qTs", tiles, pb,
                                  oaT[pb:pb + 64, hp, TOWN + s_ * 64:TOWN + (s_ + 1) * 64], "oaT")
                flush()
            T.barrier()

        xs = ExitStack()
        xT = sb(xs, "xT", [128, 8, TOWN + NSMP], F32)
        FN["xT"] = xT

        def dump_x(tag):
            if os.environ.get("KDBG") == tag:
                dma("sp", dbg["x"], xT.rearrange("p k t -> p (k t)"), r=["xT"], w=["dbg_x"])

        FN["dump_x"] = dump_x

        def mixer_out(W, Wn, srcs, l, ph):
            ranges = [(t * 512, (t + 1) * 512, 0) for t in range(TOWN // 512)] + ([(TOWN, TOWN + 64, 1), (TOWN + 64, TOWN + 128, 2)] if is_own else [])
            ctr = 0
            for (c0, c1, b) in ranges:
                n = c1 - c0
                for m in range(8):
                    pi = ctr % 2; ctr += 1
                    rr = [Wn, "xsrc"]

                    def f(m=m, c0=c0, c1=c1, pi=pi, n=n):
                        ins = None
                        for kc in range(8):
                            ins = TE.matmul(PS[pi][:, :n], lhsT=W[:, kc, m * 128:(m + 1) * 128], rhs=srcs(kc, c0, c1), start=(kc == 0), stop=(kc == 7))
                        return ins
                    op(PE, f, r=rr, w=[PSN[pi]])
                    op(DVE, lambda m=m, c0=c0, c1=c1, pi=pi, n=n, b=b: V.scalar_tensor_tensor(
                        out=xT[:, m, c0:c1], in0=PS[pi][:, :n], scalar=MOD(l, 2, m, b), in1=xT[:, m, c0:c1], op0=ALU.mult, op1=ALU.add),
                        r=[PSN[pi], "mod", "xT"], w=["xT"])

        if STAGE >= 3:
          with ExitStack() as ph:
            woe = sb(ph, "woe", [128, 8, D], BF16)
            if "woe" not in WCACHED:
                dma("pool", woe, w_out_even.rearrange("(k p) c -> p k c", p=128), w=["woe"])
                dma("sp", woe_s.rearrange("p (k c) -> p k c", c=D), woe, r=["woe"], w=["woe_s"])
                WCACHED.add("woe")
            else:
                dma("sp", woe, woe_s.rearrange("p (k c) -> p k c", c=D), r=["woe_s"], w=["woe"])
            for t in range(TOWN // 512):
                load_xT(xsrc_rows[t * 512:(t + 1) * 512, :], 4, xT, "xT", t * 512, 2, 3)
            if is_own:
                load_xT(x_smp, 1, xT, "xT", TOWN, 2, 3)
            op(POOL, lambda: G.memset(tmpv[:, 60:61], 0.0), r=["oaT", "obT"], w=["xsrc"])

            def srcs0(kc, c0, c1):
                return oaT[:, kc, c0:c1] if kc < 4 else obT[:, kc - 4, c0:c1]
            mixer_out(woe, "woe", srcs0, 0, ph)
            dump_x("xa0")
            T.barrier()


        T.barrier()
        ls.close()

        def ffn(l, with_smp=True):
            with ExitStack() as ph:
                hn2 = sb(ph, "hn2", [128, 8, 1088], BF16)
                aT = sb(ph, "aT", [128, 22, 1088], BF16)
                wgb = [sb(ph, "wgb%d" % i, [128, 8, 256], BF16) for i in range(2)]
                wub = [sb(ph, "wub%d" % i, [128, 8, 256], BF16) for i in range(2)]
                wdb = [sb(ph, "wdb%d" % i, [128, 22, 256], BF16) for i in range(2)]
                sgl = ntmp
                wgv = ffn_wg[l].rearrange("(k p) c -> p k c", p=128)
                wuv = ffn_wu[l].rearrange("(k p) c -> p k c", p=128)
                wdv = ffn_wd[l].rearrange("(c p) m -> p c m", p=128)
                gctr = 0
                dctr = 0
                ectr = 0
                for half in range(2):
                    ranges = [(0, 512, half * 1024, 0), (512, 1024, half * 1024 + 512, 0)] + ([(1024, 1088, TOWN + half * 64, 1 + half)] if with_smp else [])
                    for (lc0, lc1, xc0, b) in ranges:
                        n = lc1 - lc0
                        norm_tile(xT[:, :, xc0:xc0 + n], "xT", n, [(0, n, l, b)], hn2[:, :, lc0:lc1], "hn2", 6, 3, 4)
                    for g in range(11):
                        gi = gctr % 2; gctr += 1
                        ck = ("ffn", l, "gu", g)
                        if ck not in WCACHED:
                            dma("pool", wgb[gi], wgv[:, :, g * 256:(g + 1) * 256], r=["wg_s%d" % g], w=["wgb%d" % gi])
                            dma("pool", wub[gi], wuv[:, :, g * 256:(g + 1) * 256], r=["wu_s%d" % g], w=["wub%d" % gi])
                            dma("sp", wg_s[g].rearrange("p (k c) -> p k c", c=256), wgb[gi], r=["wgb%d" % gi], w=["wg_s%d" % g])
                            dma("sp", wu_s[g].rearrange("p (k c) -> p k c", c=256), wub[gi], r=["wub%d" % gi], w=["wu_s%d" % g])
                            WCACHED.add(ck)
                        else:
                            dma("sp", wgb[gi], wg_s[g].rearrange("p (k c) -> p k c", c=256), r=["wg_s%d" % g], w=["wgb%d" % gi])
                            dma("sp", wub[gi], wu_s[g].rearrange("p (k c) -> p k c", c=256), r=["wu_s%d" % g], w=["wub%d" % gi])
                        for i2 in range(2):
                            c = 2 * g + i2
                            for (lc0, lc1, xc0, b) in ranges:
                                n = lc1 - lc0
                                e2 = ectr % 2; ectr += 1
                                pa, pbk = e2, 2 + e2

                                def fg(wt, pi, i2=i2, lc0=lc0, lc1=lc1, n=n):
                                    ins = None
                                    for k in range(8):
                                        ins = TE.matmul(PS[pi][:, :n], lhsT=wt[:, k, i2 * 128:(i2 + 1) * 128], rhs=hn2[:, k, lc0:lc1], start=(k == 0), stop=(k == 7))
                                    return ins
                                op(PE, lambda fg=fg, gi=gi, pa=pa: fg(wgb[gi], pa), r=["wgb%d" % gi, "hn2"], w=[PSN[pa]])
                                op(PE, lambda fg=fg, gi=gi, pbk=pbk: fg(wub[gi], pbk), r=["wub%d" % gi, "hn2"], w=[PSN[pbk]])
                                sg_ = sgl[e2]; sgn = "ntmp%d" % e2
                                op(ACT, lambda sg_=sg_, pa=pa, n=n: S.activation(out=sg_[:, :n], in_=PS[pa][:, :n], func=AF.Silu), r=[PSN[pa]], w=[sgn])
                                op(DVE, lambda sg_=sg_, pbk=pbk, n=n, c=c, lc0=lc0, lc1=lc1: V.tensor_tensor(out=aT[:, c, lc0:lc1], in0=sg_[:, :n], in1=PS[pbk][:, :n], op=ALU.mult),
                                   r=[sgn, PSN[pbk]], w=["aT"])
                    for mp in range(4):
                        di = dctr % 2; dctr += 1
                        ck = ("ffn", l, "d", mp)
                        if ck not in WCACHED:
                            dma("pool", wdb[di], wdv[:, :, mp * 256:(mp + 1) * 256], r=["wd_s%d" % mp], w=["wdb%d" % di])
                            dma("sp", wd_s[mp].rearrange("p (k c) -> p k c", c=256), wdb[di], r=["wdb%d" % di], w=["wd_s%d" % mp])
                            WCACHED.add(ck)
                        else:
                            dma("sp", wdb[di], wd_s[mp].rearrange("p (k c) -> p k c", c=256), r=["wd_s%d" % mp], w=["wdb%d" % di])
                        for i2 in range(2):
                            m = 2 * mp + i2
                            for (lc0, lc1, xc0, b) in ranges:
                                n = lc1 - lc0
                                e2 = ectr % 2; ectr += 1
                                po = 4 + e2

                                def fd(di=di, i2=i2, lc0=lc0, lc1=lc1, n=n, po=po):
                                    ins = None
                                    for c in range(22):
                                        ins = TE.matmul(PS[po][:, :n], lhsT=wdb[di][:, c, i2 * 128:(i2 + 1) * 128], rhs=aT[:, c, lc0:lc1], start=(c == 0), stop=(c == 21))
                                    return ins
                                op(PE, fd, r=["wdb%d" % di, "aT"], w=[PSN[po]])
                                op(DVE, lambda m=m, xc0=xc0, n=n, po=po, b=b: V.scalar_tensor_tensor(
                                    out=xT[:, m, xc0:xc0 + n], in0=PS[po][:, :n], scalar=MOD(l, 5, m, b), in1=xT[:, m, xc0:xc0 + n], op0=ALU.mult, op1=ALU.add),
                                    r=[PSN[po], "mod", "xT"], w=["xT"])
                T.barrier()

        FN["ffn"] = ffn
        if STAGE >= 4:
            ffn(0, is_own)
            if is_own:
                dump_x("xb0")

        if STAGE >= 5:
          with ExitStack() as ph:
            wio = sb(ph, "wio", [128, 8, 2048], BF16)
            wv_ = w_in_odd.rearrange("(k p) c -> p k c", p=128)
            wio_s3 = wio_s.rearrange("p (k c) -> p k c", c=2048)
            for g in range(4):
                if "wio" not in WCACHED:
                    dma("pool", wio[:, :, g * 512:(g + 1) * 512], wv_[:, :, g * 512:(g + 1) * 512], w=["wio%d" % g])
                    dma("sp", wio_s3[:, :, g * 512:(g + 1) * 512], wio[:, :, g * 512:(g + 1) * 512], r=["wio%d" % g], w=["wio_s"])
                else:
                    dma("sp", wio[:, :, g * 512:(g + 1) * 512], wio_s3[:, :, g * 512:(g + 1) * 512], r=["wio_s"], w=["wio%d" % g])
            WCACHED.add("wio")
            waw = sb(ph, "waw", [128, 8, 128], BF16)
            wxw = sb(ph, "wxw", [128, 8, 128], BF16)
            dma("pool", waw, lru_wa.rearrange("h i j -> i h j"), w=["waw"])
            dma("pool", wxw, lru_wx.rearrange("h i j -> i h j"), w=["wxw"])
            woo = sb(ph, "woo", [128, 8, D], BF16)
            if is_own:
                dma("pool", woo, w_out_odd.rearrange("(k p) c -> p k c", p=128), w=["woo"])
            hn1 = sb(ph, "hn1", [128, 8, 512], BF16)
            yT = sb(ph, "yT", [128, 8, 512], BF16)
            LB = {}
            for nm in ("xbr",):
                LB[nm] = [sb(ph, "l_%s%d" % (nm, i), [128, 520], F32) for i in range(2)]
            for nm in ("xc", "rr", "gi", "aa", "uu", "hh", "gg"):
                LB[nm] = [sb(ph, "l_%s%d" % (nm, i), [128, 512], F32) for i in range(2)]
            LB["xcb"] = [sb(ph, "l_xcb%d" % i, [128, 512], BF16) for i in range(2)]
            f0v = sb(ph, "f0v", [128, 2], F32)
            f0c = flg[:, 24 + j:25 + j]
            op(DVE, lambda: V.tensor_scalar(out=f0v[:, 0:1], in0=f0c, scalar1=-1.0, scalar2=1.0, op0=ALU.mult, op1=ALU.add), r=["flg"], w=["f0v"])
            lctr = [0]

            def l1_tile(c0, n, b, p, sq_, first):
                norm_tile(xT[:, :, c0:c0 + n], "xT", n, [(0, n, 1, b)], hn1, "hn1", 6, 0, 1)
                l_lists = []
                for h in range(8):
                    cur = []
                    l_lists.append(cur)
                    OP[0] = (lambda eng, fn, r=(), w=(), cur=cur: cur.append((eng, fn, r, w)))
                    i2 = lctr[0] % 2; lctr[0] += 1
                    B = {k_: LB[k_][i2] for k_ in LB}
                    Bn = {k_: "l_%s%d" % (k_, i2) for k_ in LB}
                    pa = i2

                    def fx(h=h, pa=pa, col0=1024):
                        ins = None
                        for k in range(8):
                            ins = TE.matmul(PS[pa][:, :n], lhsT=wio[:, k, col0 + h * 128:col0 + (h + 1) * 128], rhs=hn1[:, k, :n], start=(k == 0), stop=(k == 7))
                        return ins
                    op(PE, fx, r=["hn1", "wio%d" % ((1024 + h * 128) // 512)], w=[PSN[pa]])
                    op(ACT, lambda B=B, pa=pa: S.copy(out=B["xbr"][:, 3:3 + n], in_=PS[pa][:, :n]), r=[PSN[pa]], w=[Bn["xbr"]])
                    hs = hist[:, sq_, h * 3:(h + 1) * 3]
                    hn_ = "hist%d_%d" % (sq_, h)
                    op(POOL, lambda B=B, hs=hs: G.tensor_copy(out=B["xbr"][:, 0:3], in_=hs), r=[hn_, "hist"], w=[Bn["xbr"] + "h"])
                    xr = [Bn["xbr"], Bn["xbr"] + "h"]
                    op(DVE, lambda B=B, h=h: V.tensor_scalar(out=B["xc"][:, :n], in0=B["xbr"][:, 0:n], scalar1=vec[:, V_CW + h:V_CW + h + 1], scalar2=vec[:, V_CB + h:V_CB + h + 1],
                                                             op0=ALU.mult, op1=ALU.add), r=xr + ["vec"], w=[Bn["xc"]])
                    for j in range(1, 4):
                        op(DVE, lambda B=B, h=h, j=j: V.scalar_tensor_tensor(out=B["xc"][:, :n], in0=B["xbr"][:, j:j + n], scalar=vec[:, V_CW + j * 8 + h:V_CW + j * 8 + h + 1],
                                                                             in1=B["xc"][:, :n], op0=ALU.mult, op1=ALU.add), r=xr + [Bn["xc"], "vec"], w=[Bn["xc"]])
                    op(POOL, lambda B=B, hs=hs: G.tensor_copy(out=hs, in_=B["xbr"][:, n:n + 3]), r=xr, w=[hn_])
                    op(ACT, lambda B=B: S.copy(out=B["xcb"][:, :n], in_=B["xc"][:, :n]), r=[Bn["xc"]], w=[Bn["xcb"]])
                    pr, pg = 2 + i2, 4 + i2
                    op(PE, lambda B=B, h=h, pr=pr: TE.matmul(PS[pr][:, :n], lhsT=waw[:, h, :], rhs=B["xcb"][:, :n], start=True, stop=True), r=[Bn["xcb"], "waw"], w=[PSN[pr]])
                    op(PE, lambda B=B, h=h, pg=pg: TE.matmul(PS[pg][:, :n], lhsT=wxw[:, h, :], rhs=B["xcb"][:, :n], start=True, stop=True), r=[Bn["xcb"], "wxw"], w=[PSN[pg]])
                    op(ACT, lambda B=B, h=h, pr=pr: S.activation(out=B["rr"][:, :n], in_=PS[pr][:, :n], func=AF.Sigmoid, bias=vec[:, V_BA + h:V_BA + h + 1]),
                       r=[PSN[pr], "vec"], w=[Bn["rr"]])
                    op(ACT, lambda B=B, h=h, pg=pg: S.activation(out=B["gi"][:, :n], in_=PS[pg][:, :n], func=AF.Sigmoid, bias=vec[:, V_BX + h:V_BX + h + 1]),
                       r=[PSN[pg], "vec"], w=[Bn["gi"]])
                    op(ACT, lambda B=B, h=h: S.activation(out=B["aa"][:, :n], in_=B["rr"][:, :n], func=AF.Exp, scale=clv[:, h:h + 1]), r=[Bn["rr"], "clv"], w=[Bn["aa"]])
                    op(DVE, lambda B=B: V.tensor_tensor(out=B["uu"][:, :n], in0=B["aa"][:, :n], in1=B["aa"][:, :n], op=ALU.mult), r=[Bn["aa"]], w=[Bn["uu"]])
                    op(ACT, lambda B=B: S.activation(out=B["uu"][:, :n], in_=B["uu"][:, :n], func=AF.Sqrt, scale=-1.0, bias=1.0), r=[Bn["uu"]], w=[Bn["uu"]])
                    if first:
                        op(DVE, lambda B=B: V.tensor_scalar(out=B["uu"][:, 0:1], in0=B["uu"][:, 0:1], scalar1=f0v[:, 0:1], scalar2=f0c, op0=ALU.mult, op1=ALU.add),
                           r=[Bn["uu"], "f0v", "flg"], w=[Bn["uu"]])
                    op(DVE, lambda B=B: V.tensor_tensor(out=B["uu"][:, :n], in0=B["uu"][:, :n], in1=B["gi"][:, :n], op=ALU.mult), r=[Bn["uu"], Bn["gi"]], w=[Bn["uu"]])
                    op(POOL, lambda B=B: G.tensor_tensor(out=B["uu"][:, :n], in0=B["uu"][:, :n], in1=B["xc"][:, :n], op=ALU.mult), r=[Bn["uu"], Bn["xc"]], w=[Bn["uu"]])
                    cr = carry[:, sq_, h:h + 1]
                    cn = "carry%d_%d" % (sq_, h)
                    op(DVE, lambda B=B, cr=cr: V.tensor_tensor_scan(out=B["hh"][:, :n], data0=B["aa"][:, :n], data1=B["uu"][:, :n], initial=cr, op0=ALU.mult, op1=ALU.add),
                       r=[Bn["aa"], Bn["uu"], cn, "carry"], w=[Bn["hh"]])
                    op(POOL, lambda B=B, cr=cr: G.tensor_copy(out=cr, in_=B["hh"][:, n - 1:n]), r=[Bn["hh"]], w=[cn])
                    if p == 2:
                        pq = 6 + i2
                        op(PE, lambda fx=fx, pq=pq: fx(pa=pq, col0=0), r=["hn1", "wio%d" % ((h * 128) // 512)], w=[PSN[pq]])
                        op(ACT, lambda B=B, pq=pq: S.activation(out=B["gg"][:, :n], in_=PS[pq][:, :n], func=AF.Square), r=[PSN[pq]], w=[Bn["gg"]])
                        op(DVE, lambda B=B: V.tensor_scalar(out=B["gg"][:, :n], in0=B["gg"][:, :n], scalar1=0.044715, scalar2=1.0, op0=ALU.mult, op1=ALU.add), r=[Bn["gg"]], w=[Bn["gg"]])
                        op(DVE, lambda B=B, pq=pq: V.tensor_tensor(out=B["gg"][:, :n], in0=B["gg"][:, :n], in1=PS[pq][:, :n], op=ALU.mult), r=[Bn["gg"], PSN[pq]], w=[Bn["gg"]])
                        op(ACT, lambda B=B: S.activation(out=B["gg"][:, :n], in_=B["gg"][:, :n], func=AF.Sigmoid, scale=1.5957691216057308), r=[Bn["gg"]], w=[Bn["gg"]])
                        op(DVE, lambda B=B, pq=pq: V.tensor_tensor(out=B["gg"][:, :n], in0=B["gg"][:, :n], in1=PS[pq][:, :n], op=ALU.mult), r=[Bn["gg"], PSN[pq]], w=[Bn["gg"]])
                        op(POOL, lambda B=B, h=h: G.tensor_tensor(out=yT[:, h, :n], in0=B["gg"][:, :n], in1=B["hh"][:, :n], op=ALU.mult), r=[Bn["gg"], Bn["hh"]], w=["yT"])
                OP[0] = T.op
                for h0 in range(0, 8, 2):
                    la, lb_ = l_lists[h0], l_lists[h0 + 1]
                    for i_ in range(max(len(la), len(lb_))):
                        if i_ < len(la):
                            T.op(*la[i_])
                        if i_ < len(lb_):
                            T.op(*lb_[i_])
                if p == 2:
                    for m in range(8):
                        pi = m % 2

                        def fo(m=m, pi=pi):
                            ins = None
                            for h in range(8):
                                ins = TE.matmul(PS[pi][:, :n], lhsT=woo[:, h, m * 128:(m + 1) * 128], rhs=yT[:, h, :n], start=(h == 0), stop=(h == 7))
                            return ins
                        op(PE, fo, r=["yT", "woo"], w=[PSN[pi]])
                        op(DVE, lambda m=m, pi=pi: V.scalar_tensor_tensor(out=xT[:, m, c0:c0 + n], in0=PS[pi][:, :n], scalar=MOD(1, 2, m, b), in1=xT[:, m, c0:c0 + n],
                                                                          op0=ALU.mult, op1=ALU.add), r=[PSN[pi], "mod", "xT"], w=["xT"])

            allh = ["hist%d_%d" % (q_, h) for q_ in range(3) for h in range(8)] + ["hist"]
            allc = ["carry%d_%d" % (q_, h) for q_ in range(3) for h in range(8)] + ["carry"]
            if j == NSLOT0:
                op(DVE, lambda: V.memset(hist, 0.0), w=["hist"])
                op(DVE, lambda: V.memset(carry, 0.0), w=["carry"])
            if is_own:
                for s_ in range(2):
                    dma("sp", hist[:, 1 + s_, :], st_conv[s_], r=["hist"], w=["hists%d" % s_])
                    dma("sp", carry[:, 1 + s_, :], st_lru[s_], r=["carry"], w=["carrys%d" % s_])
                op(POOL, lambda: G.memset(f0v[:, 1:2], 0.0), r=["hists0", "hists1", "carrys0", "carrys1", "hist", "carry"], w=["hist", "carry"])
            for t in range(4):
                l1_tile(t * 512, 512, 0, 2 if is_own else 1, 0, t == 0)
            if is_own:
                for s_ in range(2):
                    l1_tile(TOWN + s_ * 64, 64, 1 + s_, 2, 1 + s_, False)
                dma("sp", conv_p, hist[:, 0, :], r=allh, w=["conv_p"])
                dma("sp", lru_p, carry[:, 0, :], r=allc, w=["lru_p"])
                for s_ in range(2):
                    dma("sp", conv_s[s_], hist[:, 1 + s_, :], r=allh, w=["conv_s"])
                    dma("sp", lru_s[s_], carry[:, 1 + s_, :], r=allc, w=["lru_s"])
            else:
                op(DVE, lambda: V.tensor_scalar(out=hist[:, 0, :], in0=hist[:, 0, :], scalar1=flg[:, j:j + 1], scalar2=None, op0=ALU.mult), r=allh + ["flg"], w=["hist"])
                op(DVE, lambda: V.tensor_scalar(out=carry[:, 0, :], in0=carry[:, 0, :], scalar1=flg[:, j:j + 1], scalar2=None, op0=ALU.mult), r=allc + ["flg"], w=["carry"])
            if is_own:
                dump_x("xa1")
            T.barrier()
        if not is_own:
            xs.close()

    for j_ in range(NSLOT0, 4):
        run_slot(j_, j_ == 3)
    T.barrier()
    l0.close()
    ffn = FN["ffn"]
    xT = FN["xT"]
    dump_x = FN["dump_x"]

    if STAGE >= 6:
        ffn(1, True)
        dump_x("xb1")

    if STAGE >= 7:
      with ExitStack() as ph:
        ytm = sb(ph, "ytm", [128, 8, 512], F32)
        ysb = [sb(ph, "ysb%d" % i, [128, D], F32) for i in range(2)]
        yctr = 0
        tl_ = [(t * 512, 512, y_own, t * 512) for t in range(4)] + [(TOWN, 128, y_smp, 0)]
        for (c0, n, ydst, row0) in tl_:
            for k in range(8):
                q = sq[k % 2]; qn = "sq%d" % (k % 2)
                op(ACT, lambda k=k, q=q: S.activation(out=q[:, :n], in_=xT[:, k, c0:c0 + n], func=AF.Square), r=["xT"], w=[qn])
                op(PE, lambda k=k, q=q: TE.matmul(PS[6][:, :n], lhsT=od1024, rhs=q[:, :n], start=(k == 0), stop=(k == 7)), r=[qn, "od1024"], w=[PSN[6]])
            op(ACT, lambda: S.activation(out=rt[:, :n], in_=PS[6][:, :n], func=AF.Sqrt, bias=EPS), r=[PSN[6]], w=["rt"])
            op(DVE, lambda: V.reciprocal(out=rstd[:, :n], in_=rt[:, :n]), r=["rt"], w=["rstd"])
            for k in range(8):
                op(DVE, lambda k=k: V.scalar_tensor_tensor(out=ytm[:, k, :n], in0=xT[:, k, c0:c0 + n], scalar=vec[:, V_FN + k:V_FN + k + 1], in1=rstd[:, :n],
                                                           op0=ALU.mult, op1=ALU.mult), r=["xT", "rstd", "vec"], w=["ytm"])
            for s_ in range(n // 128):
                yb = ysb[yctr % 2]; yn = "ysb%d" % (yctr % 2); yctr += 1
                for half in range(2):
                    pi = half

                    def trf(s_=s_, half=half, pi=pi):
                        ins = None
                        for kk in range(4):
                            ins = TE.transpose(PS[pi][:, kk * 128:(kk + 1) * 128], ytm[:, half * 4 + kk, s_ * 128:(s_ + 1) * 128], ident_f)
                        return ins
                    op(PE, trf, r=["ytm", "cstA"], w=[PSN[pi]])
                    if half == 0:
                        op(ACT, lambda yb=yb, pi=pi: S.copy(out=yb[:, 0:512], in_=PS[pi][:, :]), r=[PSN[pi]], w=[yn])
                    else:
                        op(DVE, lambda yb=yb, pi=pi: V.tensor_copy(out=yb[:, 512:1024], in_=PS[pi][:, :]), r=[PSN[pi]], w=[yn])
                dma("sp", ydst[row0 + s_ * 128:row0 + (s_ + 1) * 128, :], yb, r=[yn], w=["y_out"])
        T.barrier()

    T.barrier()
    es.close()
    print("instructions:", T.ninst, "sems:", T.nsem)
    return nc


def _prep_inputs(inp):
    f32 = np.float32
    g = lambda k: np.ascontiguousarray(np.asarray(inp[k], dtype=f32))
    xP, xS = g("x_prompt"), g("x_sample")
    cP, cS = g("c_prompt"), g("c_sample")
    ck, cv = g("cache_sb_k")[0].reshape(16, PAST, 512), g("cache_sb_v")[0].reshape(16, PAST, 512)
    sth, stc, stl = g("state_hgrn")[0], g("state_conv")[0], g("state_lru")[0]

    def fm(v, nchunk):
        return np.ascontiguousarray(v.reshape(nchunk, 128).T)

    vec = np.zeros((128, NV), f32)
    nm, nf, ba = g("norm_mix"), g("norm_ffn"), g("b_ada")
    for l in range(2):
        vec[:, V_NMIX + l * 8:V_NMIX + l * 8 + 8] = fm(nm[l], 8)
        vec[:, V_NFFN + l * 8:V_NFFN + l * 8 + 8] = fm(nf[l], 8)
        vec[:, V_BADA + l * 48:V_BADA + (l + 1) * 48] = fm(ba[l], 48)
    vec[:, V_GN:V_GN + 4] = fm(g("hg_gnorm")[0], 4)
    lg = g("hg_lb_logits")
    for i in range(3):
        vec[:, V_LB + i * 4:V_LB + i * 4 + 4] = fm(lg[i], 4)
    cw = g("conv_w")[0]
    for j in range(4):
        vec[:, V_CW + j * 8:V_CW + j * 8 + 8] = fm(cw[j], 8)
    vec[:, V_CB:V_CB + 8] = fm(g("conv_b")[0], 8)
    vec[:, V_BA:V_BA + 8] = fm(g("lru_ba")[0], 8)
    vec[:, V_BX:V_BX + 8] = fm(g("lru_bx")[0], 8)
    vec[:, V_LAM:V_LAM + 8] = fm(g("lru_lambda")[0], 8)
    vec[:, V_FN:V_FN + 8] = fm(g("final_norm"), 8)

    cst = np.zeros((128, NCST), f32)
    cst[:, C_ID:C_ID + 128] = np.eye(128, dtype=f32)
    q = np.arange(128)[:, None]
    k = np.arange(512)[None, :]
    for m in range(4):
        cst[:, C_MASK + m * 512:C_MASK + (m + 1) * 512] = np.where(k < 128 * m + q, 0.0, NEG)
    s_ = np.arange(64)[:, None]
    t_ = np.arange(64)[None, :]
    cst[0:64, C_CM:C_CM + 64] = (s_ <= t_).astype(f32)
    cst[:, C_RS:C_RS + 512] = (np.arange(512) % 64 != 0).astype(f32)[None, :]

    shared = {
        "w_ada": g("w_ada"), "w_in_even": g("w_in_even")[0], "w_out_even": g("w_out_even")[0],
        "w_in_odd": g("w_in_odd")[0], "lru_wa": g("lru_wa")[0], "lru_wx": g("lru_wx")[0], "w_out_odd": g("w_out_odd")[0],
        "ffn_wg": g("ffn_wg"), "ffn_wu": g("ffn_wu"), "ffn_wd": g("ffn_wd"), "cst": cst,
    }
    in_maps = []
    for c in range(NCORE):
        b, r = c // 4, c % 4
        m = dict(shared)
        m["x_own"] = np.ascontiguousarray(xP[b, r * TOWN:(r + 1) * TOWN])
        xp = np.zeros((TPRE, D), f32)
        fl = np.zeros((128, 32), f32)
        for j in range(3):
            blk = r - 3 + j
            if blk >= 0:
                xp[j * TOWN:(j + 1) * TOWN] = xP[b, blk * TOWN:(blk + 1) * TOWN]
                fl[:, j] = 1.0
            else:
                fl[:, 3 + j] = NEG
        fl[:, 6] = 1.0 if r == 0 else 0.0
        fl[:, 24 + (3 - r)] = 1.0
        for i in range(NCORE):
            if i // 4 == b and i < c:
                fl[:, 8 + i] = 1.0
            if i == c - 1 and r > 0:
                fl[:, 16 + i] = 1.0
        m["x_pre"] = xp
        m["flags"] = fl
        m["x_smp"] = np.ascontiguousarray(xS[2 * c:2 * c + 2].reshape(NSMP, D))
        m["cache_k"] = np.ascontiguousarray(ck[2 * c:2 * c + 2])
        m["cache_v"] = np.ascontiguousarray(cv[2 * c:2 * c + 2])
        m["st_hgrn"] = np.ascontiguousarray(sth[2 * c:2 * c + 2])
        m["st_conv"] = np.ascontiguousarray(stc[2 * c:2 * c + 2].reshape(2, 3, 8, 128).transpose(0, 3, 2, 1).reshape(2, 128, 24))
        m["st_lru"] = np.ascontiguousarray(stl[2 * c:2 * c + 2].reshape(2, 8, 128).transpose(0, 2, 1))
        v = vec.copy()
        cs = [cP[b], cS[2 * c], cS[2 * c + 1]]
        for bi in range(3):
            cm = fm(cs[bi], 8)
            for kk in range(8):
                v[:, V_CT + kk * 4 + bi] = cm[:, kk]
        m["vec"] = v
        for kname, st in NEED_STAGE.items():
            if st > STAGE:
                m.pop(kname, None)
        in_maps.append(m)
    return in_maps


_NC = None
DBG_OUT = None


def kernel(**inputs):
    global _NC
    in_maps = _prep_inputs(inputs)
    if _NC is None:
        _NC = build()
    res = run_bass_kernel_spmd(_NC, in_maps, core_ids=list(range(NCORE)))
    R = res.results
    f32 = np.float32
    global DBG_OUT
    if os.environ.get("KDBG"):
        DBG_OUT = [R[c]["dbg_x"] for c in range(NCORE)]
    y_p = np.zeros((2, 8192, D), f32); k_p = np.zeros((1, 2, 8192, 8, 64), f32); v_p = np.zeros_like(k_p)
    y_s = np.zeros((16, 64, D), f32); k_s = np.zeros((1, 16, 64, 8, 64), f32); v_s = np.zeros_like(k_s)
    S_p = np.zeros((1, 2, 4, 128, 128), f32); conv_p = np.zeros((1, 2, 3, D), f32); h_p = np.zeros((1, 2, D), f32)
    S_s = np.zeros((1, 16, 4, 128, 128), f32); conv_s = np.zeros((1, 16, 3, D), f32); h_s = np.zeros((1, 16, D), f32)
    for c in range(NCORE):
        b, r = c // 4, c % 4
        o = R[c]
        o = {kk: o.get(kk, np.zeros(1, f32)) for kk in ("y_own","y_smp","k_own","v_own","S_p","conv_p","lru_p","k_smp","v_smp","S_s","conv_s","lru_s")} if STAGE < 9 else o
        sl = slice(r * TOWN, (r + 1) * TOWN)
        y_p[b, sl] = o["y_own"]
        k_p[0, b, sl] = o["k_own"].reshape(TOWN, 8, 64)
        v_p[0, b, sl] = o["v_own"].reshape(TOWN, 8, 64)
        y_s[2 * c:2 * c + 2] = o["y_smp"].reshape(2, 64, D)
        k_s[0, 2 * c:2 * c + 2] = o["k_smp"].reshape(2, 64, 8, 64)
        v_s[0, 2 * c:2 * c + 2] = o["v_smp"].reshape(2, 64, 8, 64)
        S_s[0, 2 * c:2 * c + 2] = o["S_s"]
        if STAGE >= 5:
            conv_s[0, 2 * c:2 * c + 2] = o["conv_s"].reshape(2, 128, 8, 3).transpose(0, 3, 2, 1).reshape(2, 3, D)
            h_s[0, 2 * c:2 * c + 2] = o["lru_s"].reshape(2, 128, 8).transpose(0, 2, 1).reshape(2, D)
        if r == 3:
            S_p[0, b] = o["S_p"]
            if STAGE >= 5:
                conv_p[0, b] = o["conv_p"].reshape(128, 8, 3).transpose(2, 1, 0).reshape(3, D)
                h_p[0, b] = o["lru_p"].reshape(128, 8).T.reshape(D)
    return (y_p, y_s, k_p, v_p, S_p, conv_p, h_p, k_s, v_s, S_s, conv_s, h_s)
```

```python
import numpy as np
from contextlib import ExitStack
import concourse.bass as bass
import concourse.mybir as mybir
from concourse.bass_utils import run_bass_kernel_spmd

F32 = mybir.dt.float32
BF16 = mybir.dt.bfloat16
AF = mybir.ActivationFunctionType
ALU = mybir.AluOpType

NCORE = 8
D = 1024
TOWN = 2048
TPRE = 6144
NSMP = 128
PAST = 4096
DFF = 2816
EPS = 1e-6
NEG = -30000.0
import os
STAGE = int(os.environ.get('KSTAGE', '9'))
SUB = int(os.environ.get('KSUB', '99'))
NPRE = int(os.environ.get('KNPRE', '12'))
SKIP = int(os.environ.get('KSKIP', '0'))
KINDS = os.environ.get('KKINDS', 'pre,own,smp').split(',')
SAME_ENG_SYNC = bool(int(os.environ.get('KSES', '1')))

V_NMIX, V_NFFN, V_BADA, V_GN, V_LB, V_CW, V_CB, V_BA, V_BX, V_LAM, V_FN, V_CT = 0, 16, 32, 128, 132, 144, 176, 184, 192, 200, 208, 216
NV = 248
C_ID, C_MASK, C_CM, C_RS = 0, 128, 128 + 2048, 128 + 2048 + 64
NCST = 128 + 2048 + 64 + 512


NEED_STAGE = {"cache_k": 2, "cache_v": 2, "w_out_even": 3, "ffn_wg": 4, "ffn_wu": 4, "ffn_wd": 4,
              "w_in_odd": 5, "lru_wa": 5, "lru_wx": 5, "w_out_odd": 5, "st_conv": 5, "st_lru": 5}


class Tr:
    def __init__(self, nc):
        self.nc = nc
        self.E = {"pe": nc.tensor, "act": nc.scalar, "dve": nc.vector, "pool": nc.gpsimd, "sp": nc.sync}
        self.nsem = 0
        self.sem = {}
        self.waited = {e: {} for e in self.E}
        self.res = {}
        self.latest = {}
        for e in self.E:
            self._newsem(e)
        self.ring = {}
        self.rpos = {}
        for q in ("sp", "pool", "act"):
            self.ring[q] = [[self._alloc("dq%s%d" % (q, i)), 0] for i in range(12)]
            self.rpos[q] = 0
        self.ninst = 0

    def _alloc(self, name):
        self.nsem += 1
        return (name, self.nc.alloc_semaphore(name))

    def _newsem(self, e):
        self.sem[e] = [self._alloc("e%s%d" % (e, self.nsem)), 0]

    def _wait(self, eng, tok):
        peng, name, sem, val = tok
        if peng == eng:
            if eng == "pe" or not SAME_ENG_SYNC:
                return
        if self.waited[eng].get(name, 0) >= val:
            return
        self.E[eng].wait_ge(sem, val)
        self.ninst += 1
        self.waited[eng][name] = val

    def _deps(self, r, w):
        deps = []
        for x in r:
            st = self.res.get(x)
            if st and st[0]:
                deps.append(st[0])
            if st and x.startswith("ps"):
                deps.extend(st[1].values())
        for x in w:
            st = self.res.get(x)
            if st:
                if st[0]:
                    deps.append(st[0])
                deps.extend(st[1].values())
        return deps

    def _record(self, tok, r, w):
        for x in w:
            self.res[x] = [tok, {}]
        for x in r:
            st = self.res.setdefault(x, [None, {}])
            st[1][tok[1]] = tok
        self.latest[tok[1]] = tok

    def op(self, eng, fn, r=(), w=()):
        for t in self._deps(r, w):
            self._wait(eng, t)
        ins = fn()
        s = self.sem[eng]
        s[1] += 1
        ins.then_inc(s[0][1], 1)
        self.ninst += 1
        tok = (eng, s[0][0], s[0][1], s[1])
        self._record(tok, r, w)
        if s[1] >= 30000:
            self._newsem(eng)
        return tok

    def dma(self, q, out, in_, r=(), w=(), **kw):
        if (SKIP & 1) and any(x in ("kT_scr", "v_scr", "qT_scr") for x in w):
            return
        if (SKIP & 2) and any(x in ("k_out", "v_out") for x in w):
            return
        for t in self._deps(r, w):
            self._wait(q, t)
        slot = self.ring[q][self.rpos[q]]
        self.rpos[q] = (self.rpos[q] + 1) % len(self.ring[q])
        if slot[1] > 0:
            self._wait(q, (None, slot[0][0], slot[0][1], slot[1]))
        ins = self.E[q].dma_start(out=out, in_=in_, **kw)
        slot[1] += 16
        ins.then_inc(slot[0][1], 16)
        self.ninst += 1
        tok = (None, slot[0][0], slot[0][1], slot[1])
        self._record(tok, r, w)
        return tok

    def raw(self, eng, fn, r=(), w=(), inc=1):
        for t in self._deps(r, w):
            self._wait(eng, t)
        ins = fn()
        nm = self._alloc("cc%d" % self.nsem)
        ins.then_inc(nm[1])
        tok = (None, nm[0], nm[1], inc)
        self._record(tok, r, w)
        return tok

    def barrier(self):
        toks = list(self.latest.values())
        for e in self.E:
            for t in toks:
                if t[0] == e:
                    continue
                self._wait(e, t)
        self.res = {}


def build():
    nc = bass.Bass("TRN2", target_bir_lowering=False)
    T = Tr(nc)
    dma = T.dma
    OP = [T.op]

    def op(eng, fn, r=(), w=()):
        return OP[0](eng, fn, r, w)
    PE, ACT, DVE, POOL = "pe", "act", "dve", "pool"
    V, S, G, TE = nc.vector, nc.scalar, nc.gpsimd, nc.tensor

    def din(name, shape):
        if NEED_STAGE.get(name, 0) > STAGE:
            return None
        return nc.dram_tensor(name, list(shape), F32, kind="ExternalInput").ap()

    def dout(name, shape):
        return nc.dram_tensor(name, list(shape), F32, kind="ExternalOutput").ap()

    x_own = din("x_own", [TOWN, D]); x_pre = din("x_pre", [TPRE, D]); x_smp = din("x_smp", [NSMP, D])
    cache_k = din("cache_k", [2, PAST, 512]); cache_v = din("cache_v", [2, PAST, 512])
    st_hgrn = din("st_hgrn", [2, 4, 128, 128]); st_conv = din("st_conv", [2, 128, 24]); st_lru = din("st_lru", [2, 128, 8])
    vec_in = din("vec", [128, NV]); flags_in = din("flags", [128, 32]); cst_in = din("cst", [128, NCST])
    w_ada = din("w_ada", [2, D, 6 * D]); w_in_even = din("w_in_even", [D, 3584]); w_out_even = din("w_out_even", [D, D])
    w_in_odd = din("w_in_odd", [D, 2048]); lru_wa = din("lru_wa", [8, 128, 128]); lru_wx = din("lru_wx", [8, 128, 128])
    w_out_odd = din("w_out_odd", [D, D])
    ffn_wg = din("ffn_wg", [2, D, DFF]); ffn_wu = din("ffn_wu", [2, D, DFF]); ffn_wd = din("ffn_wd", [2, DFF, D])

    y_own = dout("y_own", [TOWN, D]); y_smp = dout("y_smp", [NSMP, D])
    k_own = dout("k_own", [TOWN, 512]); v_own = dout("v_own", [TOWN, 512])
    S_p = dout("S_p", [4, 128, 128]); conv_p = dout("conv_p", [128, 24]); lru_p = dout("lru_p", [128, 8])
    k_smp = dout("k_smp", [NSMP, 512]); v_smp = dout("v_smp", [NSMP, 512])
    S_s = dout("S_s", [2, 4, 128, 128]); conv_s = dout("conv_s", [2, 128, 24]); lru_s = dout("lru_s", [2, 128, 8])

    kT_scr = nc.dram_tensor("kT_scr", [4, 128, TPRE + TOWN], BF16).ap()
    v_scr = nc.dram_tensor("v_scr", [4, 128, 64 * 128], BF16).ap()
    qT_scr = nc.dram_tensor("qT_scr", [4, 128, TOWN], BF16).ap()
    wie_s = nc.dram_tensor("wie_s", [128, 8 * 3584], BF16).ap()
    woe_s = nc.dram_tensor("woe_s", [128, 8 * D], BF16).ap()
    wio_s = nc.dram_tensor("wio_s", [128, 8 * 2048], BF16).ap()
    woo_s = nc.dram_tensor("woo_s", [128, 8 * D], BF16).ap()
    wg_s = nc.dram_tensor("wg_s", [11, 128, 8 * 256], BF16).ap()
    wu_s = nc.dram_tensor("wu_s", [11, 128, 8 * 256], BF16).ap()
    wd_s = nc.dram_tensor("wd_s", [4, 128, 22 * 256], BF16).ap()
    WCACHED = set()

    es = ExitStack()
    AWORDS = 53200
    arena = nc.alloc_sbuf_tensor("arena", [128, AWORDS], F32).ap()
    free_list = [[0, AWORDS]]
    live = {}

    def a_free(name):
        off, n = live.pop(name)
        free_list.append([off, n])
        free_list.sort()
        i = 0
        while i + 1 < len(free_list):
            if free_list[i][0] + free_list[i][1] == free_list[i + 1][0]:
                free_list[i][1] += free_list[i + 1][1]
                del free_list[i + 1]
            else:
                i += 1

    def sb(stack, name, shape, dt):
        shape = list(shape)
        per = 1
        for d_ in shape[1:]:
            per *= d_
        nw = per if dt == F32 else (per + 1) // 2
        nw = (nw + 7) // 8 * 8
        for fl_ in free_list:
            if fl_[1] >= nw:
                off = fl_[0]
                fl_[0] += nw
                fl_[1] -= nw
                break
        else:
            raise RuntimeError("arena full allocating %s (%d words); free=%s" % (name, nw, free_list))
        live[name] = (off, nw)
        stack.callback(a_free, name)
        base = arena[0:shape[0], off:off + nw]
        if dt != F32:
            base = base.bitcast(dt)
        base = base[:, 0:per]
        if len(shape) == 3:
            base = base.rearrange("p (a b) -> p a b", b=shape[2])
        elif len(shape) == 4:
            base = base.rearrange("p (a b c) -> p a b c", b=shape[2], c=shape[3])
        return base

    PS = [nc.alloc_psum_tensor("ps%d" % i, [128, 512], F32).ap() for i in range(8)]
    PSN = ["ps%d" % i for i in range(8)]

    vec = sb(es, "vec", [128, NV], F32)
    flg = sb(es, "flg", [128, 32], F32)
    l0 = ExitStack()
    cstA = sb(es, "cstA", [128, 128], F32)
    dma("sp", vec, vec_in, w=["vec"])
    dma("sp", flg, flags_in, w=["flg"])
    dma("sp", cstA, cst_in[:, C_ID:C_ID + 128], w=["cstA"])
    ident_f = cstA
    ident_b = sb(es, "ident_b", [128, 128], BF16)
    op(ACT, lambda: S.copy(out=ident_b, in_=ident_f), r=["cstA"], w=["ident_b"])
    od1024 = sb(es, "od1024", [128, 128], BF16)
    od128 = sb(es, "od128", [128, 128], BF16)
    op(DVE, lambda: V.memset(od1024, 1.0 / 1024.0), w=["od1024"])
    op(DVE, lambda: V.memset(od128, 1.0 / 128.0), w=["od128"])

    mT = sb(es, "mT", [128, 2 * 48 * 4], F32)
    mod = sb(es, "mod", [128, 2 * 6 * 8 * 4], F32)
    scT = sb(es, "scT", [128, 32], F32)
    lbv = sb(es, "lbv", [128, 16], F32)
    clv = sb(es, "clv", [128, 8], F32)
    tmpv = sb(es, "tmpv", [128, 64], F32)

    def MOD(l, j, k, b):
        o = ((l * 6 + j) * 8 + k) * 4 + b
        return mod[:, o:o + 1]

    op(ACT, lambda: S.activation(out=scT, in_=vec[:, V_CT:V_CT + 32], func=AF.Silu), r=["vec"], w=["scT"])
    with ExitStack() as ph:
        wad = [sb(ph, "wad%d" % i, [128, 8, 512], F32) for i in range(2)]
        it = 0
        for l in range(2):
            wv = w_ada[l].rearrange("(k p) c -> p k c", p=128)
            for g in range(12):
                wb = wad[it % 2]; wn = "wad%d" % (it % 2)
                dma("sp", wb, wv[:, :, g * 512:(g + 1) * 512], w=[wn])
                psb = PS[it % 2]; pn = PSN[it % 2]

                def mm(wb=wb, psb=psb):
                    ins = None
                    for cl in range(4):
                        for k in range(8):
                            ins = TE.matmul(psb[:, cl * 4:cl * 4 + 4], lhsT=wb[:, k, cl * 128:(cl + 1) * 128],
                                            rhs=scT[:, k * 4:k * 4 + 4], start=(k == 0), stop=(k == 7))
                    return ins
                op(PE, mm, r=[wn, "scT"], w=[pn])
                o0 = (l * 48 + g * 4) * 4
                op(DVE, lambda psb=psb, o0=o0: V.tensor_copy(out=mT[:, o0:o0 + 16], in_=psb[:, 0:16]), r=[pn], w=["mT"])
                it += 1
        T.barrier()
    for l in range(2):
        mv = mT[:, l * 192:(l + 1) * 192].rearrange("p (c b) -> p c b", b=4)
        bv = vec[:, V_BADA + l * 48:V_BADA + (l + 1) * 48].unsqueeze(2).to_broadcast([128, 48, 4])
        op(DVE, lambda mv=mv, bv=bv: V.tensor_tensor(out=mv, in0=mv, in1=bv, op=ALU.add), r=["vec", "mT"], w=["mT"])

        def m4(i, l=l):
            return mT[:, l * 192 + i * 32:l * 192 + (i + 1) * 32]

        def mo(j, l=l):
            return mod[:, (l * 6 + j) * 32:(l * 6 + j + 1) * 32]
        nmx = vec[:, V_NMIX + l * 8:V_NMIX + l * 8 + 8].unsqueeze(2).to_broadcast([128, 8, 4])
        nff = vec[:, V_NFFN + l * 8:V_NFFN + l * 8 + 8].unsqueeze(2).to_broadcast([128, 8, 4])
        v3 = lambda a: a.rearrange("p (k b) -> p k b", b=4)
        op(DVE, lambda: V.scalar_tensor_tensor(out=v3(mo(0)), in0=v3(m4(1)), scalar=1.0, in1=nmx, op0=ALU.add, op1=ALU.mult), r=["mT", "vec"], w=["mod"])
        op(DVE, lambda: V.tensor_copy(out=mo(1), in_=m4(0)), r=["mT"], w=["mod"])
        op(DVE, lambda: V.tensor_scalar(out=mo(2), in0=m4(2), scalar1=1.0, scalar2=None, op0=ALU.add), r=["mT"], w=["mod"])
        op(DVE, lambda: V.scalar_tensor_tensor(out=v3(mo(3)), in0=v3(m4(4)), scalar=1.0, in1=nff, op0=ALU.add, op1=ALU.mult), r=["mT", "vec"], w=["mod"])
        op(DVE, lambda: V.tensor_copy(out=mo(4), in_=m4(3)), r=["mT"], w=["mod"])
        op(DVE, lambda: V.tensor_scalar(out=mo(5), in0=m4(5), scalar1=1.0, scalar2=None, op0=ALU.add), r=["mT"], w=["mod"])
    op(ACT, lambda: S.activation(out=tmpv[:, 0:12], in_=vec[:, V_LB:V_LB + 12], func=AF.Exp), r=["vec"], w=["tmpv"])
    op(DVE, lambda: V.tensor_tensor(out=tmpv[:, 12:16], in0=tmpv[:, 0:4], in1=tmpv[:, 4:8], op=ALU.add), r=["tmpv"], w=["tmpv"])
    op(DVE, lambda: V.tensor_tensor(out=tmpv[:, 12:16], in0=tmpv[:, 12:16], in1=tmpv[:, 8:12], op=ALU.add), r=["tmpv"], w=["tmpv"])
    op(DVE, lambda: V.reciprocal(out=tmpv[:, 16:20], in_=tmpv[:, 12:16]), r=["tmpv"], w=["tmpv"])
    op(DVE, lambda: V.tensor_tensor(out=lbv[:, 0:4], in0=tmpv[:, 0:4], in1=tmpv[:, 16:20], op=ALU.mult), r=["tmpv"], w=["lbv"])
    op(DVE, lambda: V.tensor_scalar(out=lbv[:, 4:8], in0=lbv[:, 0:4], scalar1=-1.0, scalar2=1.0, op0=ALU.mult, op1=ALU.add), r=["lbv"], w=["lbv"])
    op(DVE, lambda: V.tensor_scalar(out=lbv[:, 8:12], in0=lbv[:, 0:4], scalar1=1.0, scalar2=-1.0, op0=ALU.mult, op1=ALU.add), r=["lbv"], w=["lbv"])
    op(ACT, lambda: S.activation(out=tmpv[:, 32:40], in_=vec[:, V_LAM:V_LAM + 8], func=AF.Exp, scale=-1.0), r=["vec"], w=["tmpv"])
    op(ACT, lambda: S.activation(out=tmpv[:, 40:48], in_=tmpv[:, 32:40], func=AF.Ln, bias=1.0), r=["tmpv"], w=["tmpv"])
    op(DVE, lambda: V.tensor_scalar(out=clv, in0=tmpv[:, 40:48], scalar1=-8.0, scalar2=None, op0=ALU.mult), r=["tmpv"], w=["clv"])

    wk = es
    sq = [sb(wk, "sq%d" % i, [128, 512], BF16) for i in range(2)]
    rt = sb(wk, "rt", [128, 512], F32)
    rstd = sb(wk, "rstd", [128, 512], F32)
    ntmp = [sb(wk, "ntmp%d" % i, [128, 512], F32) for i in range(2)]

    def norm_tile(src, srcn, N, parts, dst, dstn, psi, jg, jsh):
        def stat():
            ins = None
            return ins
        for k in range(8):
            q = sq[k % 2]; qn = "sq%d" % (k % 2)
            op(ACT, lambda k=k, q=q: S.activation(out=q[:, :N], in_=src[:, k, :N], func=AF.Square), r=[srcn], w=[qn])
            op(PE, lambda k=k, q=q: TE.matmul(PS[psi][:, :N], lhsT=od1024, rhs=q[:, :N], start=(k == 0), stop=(k == 7)),
               r=[qn, "od1024"], w=[PSN[psi]])
        op(ACT, lambda: S.activation(out=rt[:, :N], in_=PS[psi][:, :N], func=AF.Sqrt, bias=EPS), r=[PSN[psi]], w=["rt"])
        op(DVE, lambda: V.reciprocal(out=rstd[:, :N], in_=rt[:, :N]), r=["rt"], w=["rstd"])
        for k in range(8):
            tb = ntmp[k % 2]; tn = "ntmp%d" % (k % 2)
            op(DVE, lambda k=k, tb=tb: V.tensor_tensor(out=tb[:, :N], in0=src[:, k, :N], in1=rstd[:, :N], op=ALU.mult), r=[srcn, "rstd"], w=[tn])
            for (c0, c1, l, b) in parts:
                op(ACT, lambda k=k, tb=tb, c0=c0, c1=c1, l=l, b=b: S.activation(
                    out=dst[:, k, c0:c1], in_=tb[:, c0:c1], func=AF.Identity, scale=MOD(l, jg, k, b), bias=MOD(l, jsh, k, b)),
                    r=[tn, "mod"], w=[dstn])

    xin = [None, None]
    xin_ctr = [0]

    def load_xT(src_rows, nsub, dst, dstn, col0, psA, psB):
        for s in range(nsub):
            i = xin_ctr[0] % 2; xin_ctr[0] += 1
            xb = xin[i]; xn = "xin%d" % i
            dma("sp", xb, src_rows[s * 128:(s + 1) * 128, :], w=[xn])
            for half in range(2):
                pi = psA if half == 0 else psB

                def tr(xb=xb, half=half, pi=pi):
                    ins = None
                    for kk in range(4):
                        k = half * 4 + kk
                        ins = TE.transpose(PS[pi][:, kk * 128:(kk + 1) * 128], xb[:, k * 128:(k + 1) * 128], ident_f)
                    return ins
                op(PE, tr, r=[xn, "cstA"], w=[PSN[pi]])
                eng = ACT if half == 0 else DVE
                dv = dst[:, half * 4:half * 4 + 4, col0 + s * 128:col0 + (s + 1) * 128]
                pv = PS[pi][:, :].rearrange("p (k t) -> p k t", t=128)
                if eng == ACT:
                    op(ACT, lambda dv=dv, pv=pv: S.copy(out=dv, in_=pv), r=[PSN[pi]], w=[dstn])
                else:
                    op(DVE, lambda dv=dv, pv=pv: V.tensor_copy(out=dv, in_=pv), r=[PSN[pi]], w=[dstn])

    Sst = sb(es, "Sst", [128, 3, 4, 128], F32)
    Sbf = sb(es, "Sbf", [128, 3, 4, 128], BF16)
    hist = sb(es, "hist", [128, 3, 24], F32)
    carry = sb(es, "carry", [128, 3, 8], F32)
    dbg = {}
    FN = {}
    if os.environ.get("KDBG"):
        dbg["x"] = nc.dram_tensor("dbg_x", [128, 8 * (TOWN + NSMP)], F32, kind="ExternalOutput").ap()

    NSLOT0 = int(os.environ.get("KSLOT0", "0"))

    def run_slot(j, is_own):
        ls = ExitStack()
        cst = sb(ls, "cst", [128, NCST], F32)
        dma("sp", cst, cst_in, w=["cst"])
        cmask = cst[0:64, C_CM:C_CM + 64]
        rsmask = cst[:, C_RS:C_RS + 512]
        xin[0] = sb(ls, "xin0", [128, D], F32); xin[1] = sb(ls, "xin1", [128, D], F32)
        xsrc_rows = x_own if is_own else x_pre[j * TOWN:(j + 1) * TOWN, :]
        obT = sb(ls, "obT", [128, 4, TOWN + NSMP], BF16)
        kTs = sb(ls, "kTs", [128, 4, NSMP], BF16)
        vs_new = sb(ls, "vs_new", [64, 2, 512], BF16)
        qTs = sb(ls, "qTs", [128, 4, NSMP], BF16)

        with ExitStack() as ph:
            wie = sb(ph, "wie", [128, 8, 3584], BF16)
            wv = w_in_even.rearrange("(k p) c -> p k c", p=128)
            wie_s3 = wie_s.rearrange("p (k c) -> p k c", c=3584)
            for g in range(7):
                if "wie" not in WCACHED:
                    dma("pool", wie[:, :, g * 512:(g + 1) * 512], wv[:, :, g * 512:(g + 1) * 512], w=["wie%d" % g])
                    dma("sp", wie_s3[:, :, g * 512:(g + 1) * 512], wie[:, :, g * 512:(g + 1) * 512], r=["wie%d" % g], w=["wie_s"])
                else:
                    dma("sp", wie[:, :, g * 512:(g + 1) * 512], wie_s3[:, :, g * 512:(g + 1) * 512], r=["wie_s"], w=["wie%d" % g])
            WCACHED.add("wie")
            WIE = ["wie%d" % g for g in range(7)]
            xTt = sb(ph, "xTt", [128, 8, 512], F32)
            hnT = [sb(ph, "hnT%d" % i, [128, 8, 512], BF16) for i in range(2)]
            stg_b = [sb(ph, "stgb%d" % i, [128, 4, 512], BF16) for i in range(2)]
            stg_f = [sb(ph, "stgf%d" % i, [128, 512], F32) for i in range(2)]
            stg_v = [sb(ph, "stgv%d" % i, [128, 512], BF16) for i in range(2)]
            ibt = sb(ph, "ibt", [64, 8, 512], BF16)
            hb = {}
            for nm in ("sg", "ff", "kk", "lf", "bb", "d3", "qs"):
                hb[nm] = [sb(ph, "h_%s0" % nm, [128, 512], F32)] * 2
            hb["d1"] = hb["d3"]
            for nm in ("qt", "qh", "kt", "kh"):
                hb[nm] = [sb(ph, "h_%s%d" % (nm, i), [128, 512], BF16) for i in range(2)]
            gsil = sb(ph, "gsil", [128, 4, 512], BF16)
            khat = [sb(ph, "khat%d" % i, [64, 8, 128], BF16) for i in range(2)]
            attT = [sb(ph, "attT%d" % i, [64, 8, 64], BF16) for i in range(2)]
            ebl = [sb(ph, "ebl%d" % i, [128, 8], F32) for i in range(2)]
            osq, ort, ors, otm = sq[0], rt, rstd, ntmp[0]
            stg_ctr = [0, 0, 0]

            if j == NSLOT0:
                op(DVE, lambda: V.memset(Sst[:, 0], 0.0), w=["S0"])
                op(POOL, lambda: G.memset(Sbf[:, 0], 0.0), w=["Sb0"])
            for s in range(2 if is_own else 0):
                dma("sp", Sst[:, 1 + s], st_hgrn[s].rearrange("h c v -> c h v"), w=["S%d" % (1 + s)])
                op(ACT, lambda s=s: S.copy(out=Sbf[:, 1 + s], in_=Sst[:, 1 + s]), r=["S%d" % (1 + s)], w=["Sb%d" % (1 + s)])

            def proj_fm(col0, hn, hnn, N, psi):
                def f():
                    ins = None
                    for k in range(8):
                        ins = TE.matmul(PS[psi][:, :N], lhsT=wie[:, k, col0:col0 + 128], rhs=hn[:, k, :N], start=(k == 0), stop=(k == 7))
                    return ins
                op(PE, f, r=[hnn, WIE[col0 // 512]], w=[PSN[psi]])

            def proj_tm(col0, hn, hnn, t0, M, psi):
                def f():
                    ins = None
                    for k in range(8):
                        ins = TE.matmul(PS[psi][:M, :512], lhsT=hn[:, k, t0:t0 + M], rhs=wie[:, k, col0:col0 + 512], start=(k == 0), stop=(k == 7))
                    return ins
                op(PE, f, r=[hnn, WIE[col0 // 512]], w=[PSN[psi]])

            def l0_tile(kind, src_rows, N, parts, key0, own0, tidx, chunk_state, flag_after):
                if kind not in KINDS:
                    return
                nsub = N // 128
                nch = N // 64
                hn = hnT[tidx % 2]; hnn = "hnT%d" % (tidx % 2)
                load_xT(src_rows, nsub, xTt, "xTt", 0, 0, 1)
                if SUB < 1:
                    return
                norm_tile(xTt, "xTt", N, parts, hn, hnn, 0, 0, 1)
                if SUB < 2:
                    return
                pj = [0]

                def nextps():
                    pj[0] += 1
                    return pj[0] % 2
                if kind in ("own", "smp"):
                    sbq = stg_b[stg_ctr[0] % 2]; sbqn = "stgb%d" % (stg_ctr[0] % 2); stg_ctr[0] += 1
                    for c4 in range(4):
                        pi = nextps()
                        proj_fm(c4 * 128, hn, hnn, N, pi)
                        dstq = sbq[:, c4, :N] if kind == "own" else qTs[:, c4, :N]
                        dn = sbqn if kind == "own" else "qTs"
                        op(ACT, lambda dstq=dstq, pi=pi: S.activation(out=dstq, in_=PS[pi][:, :N], func=AF.Copy, scale=0.125), r=[PSN[pi]], w=[dn])
                    if kind == "own":
                        dma("sp", qT_scr[:, :, own0:own0 + N].rearrange("h p t -> p h t"), sbq[:, :, :N], r=[sbqn], w=["qT_scr"])
                sbk = stg_b[stg_ctr[0] % 2]; sbkn = "stgb%d" % (stg_ctr[0] % 2); stg_ctr[0] += 1
                for c4 in range(4):
                    pi = nextps()
                    proj_fm(512 + c4 * 128, hn, hnn, N, pi)
                    dstk = sbk[:, c4, :N] if kind != "smp" else kTs[:, c4, :N]
                    dn = sbkn if kind != "smp" else "kTs"
                    op(DVE, lambda dstk=dstk, pi=pi: V.tensor_copy(out=dstk, in_=PS[pi][:, :N]), r=[PSN[pi]], w=[dn])
                if kind != "smp":
                    dma("sp", kT_scr[:, :, key0:key0 + N].rearrange("h p t -> p h t"), sbk[:, :, :N], r=[sbkn], w=["kT_scr"])
                if kind in ("own", "smp"):
                    for s in range(nsub):
                        pi = nextps()
                        proj_tm(512, hn, hnn, s * 128, 128, pi)
                        sf = stg_f[stg_ctr[1] % 2]; sfn = "stgf%d" % (stg_ctr[1] % 2); stg_ctr[1] += 1
                        op(ACT, lambda sf=sf, pi=pi: S.copy(out=sf, in_=PS[pi][:, :]), r=[PSN[pi]], w=[sfn])
                        dst = k_own[own0 + s * 128:own0 + (s + 1) * 128, :] if kind == "own" else k_smp[:, :]
                        if is_own:
                            dma("sp", dst, sf, r=[sfn], w=["k_out"])
                if kind != "smp":
                    for s in range(nsub):
                        pi = nextps()
                        proj_tm(1024, hn, hnn, s * 128, 128, pi)
                        if kind == "own":
                            sf = stg_f[stg_ctr[1] % 2]; sfn = "stgf%d" % (stg_ctr[1] % 2); stg_ctr[1] += 1
                            op(ACT, lambda sf=sf, pi=pi: S.copy(out=sf, in_=PS[pi][:, :]), r=[PSN[pi]], w=[sfn])
                            if is_own:
                                dma("sp", v_own[own0 + s * 128:own0 + (s + 1) * 128, :], sf, r=[sfn], w=["v_out"])
                        sv = stg_v[stg_ctr[2] % 2]; svn = "stgv%d" % (stg_ctr[2] % 2); stg_ctr[2] += 1
                        op(DVE, lambda sv=sv, pi=pi: V.tensor_copy(out=sv, in_=PS[pi][:, :]), r=[PSN[pi]], w=[svn])
                        kt = (key0 + s * 128) // 128
                        dma("sp", v_scr[:, :, kt * 128:(kt + 1) * 128].rearrange("h p c -> p h c"),
                            sv[:, :].rearrange("p (h c) -> p h c", c=128), r=[svn], w=["v_scr"])
                else:
                    for s in range(2):
                        pi = nextps()
                        proj_tm(1024, hn, hnn, s * 64, 64, pi)
                        sf = stg_f[stg_ctr[1] % 2]; sfn = "stgf%d" % (stg_ctr[1] % 2); stg_ctr[1] += 1
                        op(ACT, lambda sf=sf, pi=pi: S.copy(out=sf[0:64, :], in_=PS[pi][0:64, :]), r=[PSN[pi]], w=[sfn])
                        dma("sp", v_smp[s * 64:(s + 1) * 64, :], sf[0:64, :], r=[sfn], w=["v_out"])
                        op(DVE, lambda s=s, pi=pi: V.tensor_copy(out=vs_new[:, s, :], in_=PS[pi][0:64, :]), r=[PSN[pi]], w=["vs_new"])
                if SUB < 3:
                    return
                for j in range(nch):
                    pi = nextps()
                    proj_tm(2560, hn, hnn, j * 64, 64, pi)
                    eng = ACT if j % 2 == 0 else DVE
                    if eng == ACT:
                        op(ACT, lambda j=j, pi=pi: S.copy(out=ibt[:, j, :], in_=PS[pi][0:64, :]), r=[PSN[pi]], w=["ibt"])
                    else:
                        op(DVE, lambda j=j, pi=pi: V.tensor_copy(out=ibt[:, j, :], in_=PS[pi][0:64, :]), r=[PSN[pi]], w=["ibt"])
                full = kind != "pre"
                rec_lists = []
                for h in range(4):
                    cur = []
                    split = [None]
                    OP[0] = (lambda eng, fn, r=(), w=(), cur=cur: cur.append((eng, fn, r, w)))
                    i2 = h % 2
                    B = {nm: hb[nm][i2] for nm in hb}
                    Bn = {nm: "h_%s%d" % (nm, i2 if nm in ("qt", "qh", "kt", "kh") else 0) for nm in hb}
                    Bn["d1"] = Bn["d3"]
                    pi = nextps()
                    proj_fm(2048 + h * 128, hn, hnn, N, pi)
                    op(ACT, lambda pi=pi, B=B: S.activation(out=B["sg"][:, :N], in_=PS[pi][:, :N], func=AF.Sigmoid), r=[PSN[pi]], w=[Bn["sg"]])
                    op(DVE, lambda B=B, h=h: V.tensor_scalar(out=B["ff"][:, :N], in0=B["sg"][:, :N], scalar1=lbv[:, 4 + h:5 + h], scalar2=lbv[:, h:h + 1],
                                                             op0=ALU.mult, op1=ALU.add), r=[Bn["sg"], "lbv"], w=[Bn["ff"]])
                    op(POOL, lambda B=B, h=h: G.tensor_scalar(out=B["kk"][:, :N], in0=B["sg"][:, :N], scalar1=lbv[:, 8 + h:9 + h], scalar2=lbv[:, 4 + h:5 + h],
                                                              op0=ALU.mult, op1=ALU.add), r=[Bn["sg"], "lbv"], w=[Bn["kk"]])
                    op(ACT, lambda B=B: S.activation(out=B["lf"][:, :N], in_=B["ff"][:, :N], func=AF.Ln), r=[Bn["ff"]], w=[Bn["lf"]])
                    op(DVE, lambda B=B: V.tensor_tensor_scan(out=B["bb"][:, :N], data0=rsmask[:, :N], data1=B["lf"][:, :N], initial=0.0,
                                                             op0=ALU.mult, op1=ALU.add), r=[Bn["lf"], "cst"], w=[Bn["bb"]])
                    b3 = B["bb"][:, :N].rearrange("p (j t) -> p j t", t=64)
                    bL = b3[:, :, 63:64]
                    bM = b3[:, :, 31:32]
                    eb = ebl[i2]; ebn = "ebl%d" % i2
                    op(ACT, lambda eb=eb, bL=bL: S.activation(out=eb[:, :nch].unsqueeze(2), in_=bL, func=AF.Exp), r=[Bn["bb"]], w=[ebn])
                    d3v = B["d3"][:, :N].rearrange("p (j t) -> p j t", t=64)
                    op(DVE, lambda d3v=d3v, bL=bL, b3=b3: V.tensor_tensor(out=d3v, in0=bL.to_broadcast([128, nch, 64]), in1=b3, op=ALU.subtract),
                       r=[Bn["bb"]], w=[Bn["d3"]])
                    op(ACT, lambda B=B: S.activation(out=B["d3"][:, :N], in_=B["d3"][:, :N], func=AF.Exp), r=[Bn["d3"]], w=[Bn["d3"]])
                    op(POOL, lambda B=B: G.tensor_tensor(out=B["kh"][:, :N], in0=B["kk"][:, :N], in1=B["d3"][:, :N], op=ALU.mult),
                       r=[Bn["kk"], Bn["d3"]], w=[Bn["kh"]])
                    pt = 2 + i2
                    ptv = PS[pt][:, :].bitcast(BF16)

                    def trk(B=B, ptv=ptv):
                        ins = None
                        for j in range(nch):
                            ins = TE.transpose(ptv[0:64, j * 128:(j + 1) * 128], B["kh"][:, j * 64:(j + 1) * 64], ident_b)
                        return ins
                    op(PE, trk, r=[Bn["kh"], "ident_b"], w=[PSN[pt]])
                    kh = khat[i2]; khn = "khat%d" % i2
                    op(ACT, lambda kh=kh, ptv=ptv: S.copy(out=kh[:, :nch, :], in_=ptv[0:64, :nch * 128].rearrange("p (j c) -> p j c", c=128)),
                       r=[PSN[pt]], w=[khn])
                    if full:
                        d1v = B["d1"][:, :N].rearrange("p (j t) -> p j t", t=64)
                        op(DVE, lambda d1v=d1v, bM=bM, b3=b3: V.tensor_tensor(out=d1v, in0=b3, in1=bM.to_broadcast([128, nch, 64]), op=ALU.subtract),
                           r=[Bn["bb"]], w=[Bn["d1"]])
                        pq = nextps()
                        proj_fm(1536 + h * 128, hn, hnn, N, pq)
                        op(ACT, lambda pq=pq, B=B: S.activation(out=B["qs"][:, :N], in_=PS[pq][:, :N], func=AF.Silu), r=[PSN[pq]], w=[Bn["qs"]])
                        op(ACT, lambda B=B: S.activation(out=B["sg"][:, :N], in_=B["d1"][:, :N], func=AF.Exp), r=[Bn["d1"]], w=[Bn["sg"]])
                        op(DVE, lambda B=B: V.scalar_tensor_tensor(out=B["qt"][:, :N], in0=B["qs"][:, :N], scalar=128.0 ** -0.5, in1=B["sg"][:, :N],
                                                                   op0=ALU.mult, op1=ALU.mult), r=[Bn["qs"], Bn["sg"]], w=[Bn["qt"]])
                        op(ACT, lambda B=B: S.activation(out=B["ff"][:, :N], in_=B["d1"][:, :N], func=AF.Exp, scale=-1.0), r=[Bn["d1"]], w=[Bn["ff"]])
                        op(POOL, lambda B=B: G.tensor_tensor(out=B["kt"][:, :N], in0=B["kk"][:, :N], in1=B["ff"][:, :N], op=ALU.mult),
                           r=[Bn["kk"], Bn["ff"]], w=[Bn["kt"]])
                        op(ACT, lambda B=B: S.activation(out=B["lf"][:, :N], in_=B["bb"][:, :N], func=AF.Exp), r=[Bn["bb"]], w=[Bn["lf"]])
                        op(DVE, lambda B=B: V.scalar_tensor_tensor(out=B["qh"][:, :N], in0=B["qs"][:, :N], scalar=128.0 ** -0.5, in1=B["lf"][:, :N],
                                                                   op0=ALU.mult, op1=ALU.mult), r=[Bn["qs"], Bn["lf"]], w=[Bn["qh"]])
                        pg = nextps()
                        proj_fm(3072 + h * 128, hn, hnn, N, pg)
                        op(ACT, lambda pg=pg, h=h: S.activation(out=gsil[:, h, :N], in_=PS[pg][:, :N], func=AF.Silu), r=[PSN[pg]], w=["gsil"])
                        pa = 4 + i2

                        def att(B=B, pa=pa):
                            ins = None
                            for j in range(nch):
                                ins = TE.matmul(PS[pa][0:64, j * 64:(j + 1) * 64], lhsT=B["kt"][:, j * 64:(j + 1) * 64],
                                                rhs=B["qt"][:, j * 64:(j + 1) * 64], start=True, stop=True)
                            return ins
                        op(PE, att, r=[Bn["kt"], Bn["qt"]], w=[PSN[pa]])
                        at = attT[i2]; atn = "attT%d" % i2
                        op(DVE, lambda at=at, pa=pa: V.tensor_tensor(out=at[:, :nch, :], in0=PS[pa][0:64, :nch * 64].rearrange("p (j t) -> p j t", t=64),
                                                                     in1=cmask.unsqueeze(1).to_broadcast([64, nch, 64]), op=ALU.mult),
                           r=[PSN[pa], "cst"], w=[atn])
                    split[0] = len(cur)
                    po = 6 + i2
                    for j in range(nch):
                        si = chunk_state(j)
                        Sn, Sbn = "S%d_%d" % (si, h), "Sb%d_%d" % (si, h)
                        base_r = ["S%d" % si, "Sb%d" % si]
                        if full:
                            def omm(j=j, si=si, h=h, at=at, B=B, po=po):
                                TE.matmul(PS[po][:, j * 64:(j + 1) * 64], lhsT=ibt[:, j, h * 128:(h + 1) * 128], rhs=at[:, j, :], start=True, stop=False)
                                return TE.matmul(PS[po][:, j * 64:(j + 1) * 64], lhsT=Sbf[:, si, h, :], rhs=B["qh"][:, j * 64:(j + 1) * 64], start=False, stop=True)
                            op(PE, omm, r=["ibt", atn, Sbn, "Sb%d" % si, Bn["qh"]], w=[PSN[po]])
                        pd = 2 + i2
                        op(PE, lambda j=j, kh=kh, h=h, pd=pd: TE.matmul(PS[pd][:, 256 + (j % 2) * 128:256 + (j % 2) * 128 + 128], lhsT=kh[:, j, :],
                                                                        rhs=ibt[:, j, h * 128:(h + 1) * 128], start=True, stop=True),
                           r=[khn, "ibt"], w=[PSN[pd]])
                        op(DVE, lambda j=j, si=si, h=h, eb=eb, pd=pd: V.scalar_tensor_tensor(
                            out=Sst[:, si, h, :], in0=Sst[:, si, h, :], scalar=eb[:, j:j + 1], in1=PS[pd][:, 256 + (j % 2) * 128:256 + (j % 2) * 128 + 128],
                            op0=ALU.mult, op1=ALU.add), r=[PSN[pd], ebn, "S%d" % si, Sn], w=[Sn])
                        op(ACT, lambda si=si, h=h: S.copy(out=Sbf[:, si, h, :], in_=Sst[:, si, h, :]), r=[Sn], w=[Sbn])
                    if full:
                        op(ACT, lambda po=po: S.activation(out=osq[:, :N], in_=PS[po][:, :N], func=AF.Square), r=[PSN[po]], w=["sq0"])
                        pm = nextps()
                        op(PE, lambda pm=pm: TE.matmul(PS[pm][:, :N], lhsT=od128, rhs=osq[:, :N], start=True, stop=True), r=["sq0", "od128"], w=[PSN[pm]])
                        op(ACT, lambda pm=pm: S.activation(out=ort[:, :N], in_=PS[pm][:, :N], func=AF.Sqrt, bias=EPS), r=[PSN[pm]], w=["rt"])
                        op(DVE, lambda: V.reciprocal(out=ors[:, :N], in_=ort[:, :N]), r=["rt"], w=["rstd"])
                        op(DVE, lambda po=po, h=h: V.scalar_tensor_tensor(out=otm[:, :N], in0=PS[po][:, :N], scalar=vec[:, V_GN + h:V_GN + h + 1], in1=ors[:, :N],
                                                                         op0=ALU.mult, op1=ALU.mult), r=[PSN[po], "rstd", "vec"], w=["ntmp0"])
                        oc0 = own0 if kind == "own" else TOWN
                        op(POOL, lambda h=h, oc0=oc0: G.tensor_tensor(out=obT[:, h, oc0:oc0 + N], in0=otm[:, :N], in1=gsil[:, h, :N], op=ALU.mult),
                           r=["ntmp0", "gsil"], w=["obT"])
                    rec_lists.append((cur, split[0]))
                OP[0] = T.op
                preps = [c_[:s_] for (c_, s_) in rec_lists]
                recs = [c_[s_:] for (c_, s_) in rec_lists]
                for a_ in preps[0]:
                    T.op(*a_)
                for h in range(4):
                    if h + 1 < 4:
                        for a_ in preps[h + 1]:
                            T.op(*a_)
                    for a_ in recs[h]:
                        T.op(*a_)
                if flag_after is not None:
                    for h in range(4):
                        op(DVE, lambda h=h: V.tensor_scalar(out=Sst[:, 0, h, :], in0=Sst[:, 0, h, :], scalar1=flg[:, flag_after:flag_after + 1], scalar2=None,
                                                            op0=ALU.mult), r=["S0_%d" % h, "flg"], w=["S0_%d" % h])
                        op(ACT, lambda h=h: S.copy(out=Sbf[:, 0, h, :], in_=Sst[:, 0, h, :]), r=["S0_%d" % h], w=["Sb0_%d" % h])

            tix = 0
            prm = [(0, 512, 0, 0)]
            if STAGE >= 1:
                for t in range(TOWN // 512):
                    fa = j if (t == 3 and not is_own) else None
                    l0_tile("own", xsrc_rows[t * 512:(t + 1) * 512, :], 512, prm, j * TOWN + t * 512, t * 512, tix, lambda jj: 0, fa)
                    tix += 1
                if is_own:
                    dma("sp", S_p.rearrange("h c v -> c h v"), Sst[:, 0], r=["S0_%d" % h for h in range(4)], w=["S_p"])
                    l0_tile("smp", x_smp, 128, [(0, 64, 0, 1), (64, 128, 0, 2)], 0, 0, tix, lambda jj: 1 + jj, None)
                    for s in range(2):
                        dma("sp", S_s[s].rearrange("h c v -> c h v"), Sst[:, 1 + s], r=["S%d_%d" % (1 + s, h) for h in range(4)], w=["S_s"])
            T.barrier()


        oaT = sb(ls, "oaT", [128, 4, TOWN + NSMP], BF16)
        if STAGE >= 2:
          with ExitStack() as ph:
            bufA = [sb(ph, "bufA%d" % i, [128, 8192], BF16) for i in range(2)]
            bufB = [sb(ph, "bufB%d" % i, [128, 8192], BF16) for i in range(2)]
            qTb = [sb(ph, "qTb%d" % i, [128, TOWN], BF16) for i in range(2)]
            ones512 = sb(ph, "ones512", [128, 512], F32)
            op(DVE, lambda: V.memset(ones512, 1.0), w=["ones512"])
            W_ = {}
            for nm, shp, dt in (("e", [128, 512], F32), ("sp", [128, 512], F32), ("P", [128, 520], F32), ("a", [128, 512], F32),
                                ("zs", [128, 512], F32), ("w", [128, 512], BF16), ("wT", [128, 4, 128], BF16)):
                W_[nm] = [sb(ph, "at_%s%d" % (nm, i), shp, dt) for i in range(3)]
            nbuf = [sb(ph, "nbuf%d" % i, [128, 40], F32) for i in range(5)]
            for i in range(3):
                op(DVE, lambda i=i: V.memset(W_["P"][i][:, 0:1], 0.0), w=["at_P%d" % i])
            for i in range(5):
                op(DVE, lambda i=i: V.memset(nbuf[i], 0.0), w=["nb%d" % i])
            tctr = [0]
            qctr = [0]

            pipe = []
            qo = [0]

            ORDER = (("A0", 0), ("A1", 1), ("C1", 3), ("A2", 1), ("C3", 4), ("B1", 2), ("B2", 2), ("C2", 3))

            def push(job):
                pipe.append(job)
                n_ = len(pipe)
                for (st_, lag) in ORDER:
                    idx = n_ - 1 - lag
                    if idx >= 0:
                        pipe[idx][st_]()

            def flush():
                n_ = len(pipe)
                for k_ in range(1, 5):
                    for (st_, lag) in ORDER:
                        idx = n_ - 1 + k_ - lag
                        if 0 <= idx < n_:
                            pipe[idx][st_]()
                del pipe[:]

            def att_qtile(M, qT_ap, qn, tiles, pb, dst_ap, dstn):
                qi = qctr[0] % 5; qctr[0] += 1
                ob = 6 + qo[0] % 2; qo[0] += 1
                nb = nbuf[qi]
                nt = len(tiles)
                for ti, tl in enumerate(tiles):
                    sl = tctr[0] % 3
                    tb = 4 + tctr[0] % 2
                    zb = tctr[0] % 4
                    tctr[0] += 1
                    n = tl["n"]
                    B = {k_: W_[k_][sl] for k_ in W_}
                    Bn = {k_: "at_%s%d" % (k_, sl) for k_ in W_}
                    fb = tl["fb"]
                    if tl["mask"] is not None:
                        src, srcn = B["zs"][:M, :n], Bn["zs"]
                    else:
                        src, srcn = PS[zb][:M, :n], PSN[zb]
                    c0, c1 = "nb%d_%d" % (qi, ti), "nb%d_%d" % (qi, ti + 1)
                    bias_ap, bn_ = nb[:M, ti + 1:ti + 2], c1
                    nsub = (n + 127) // 128
                    ks = min(n, 128)
                    tv = PS[tb][:, :].bitcast(BF16)

                    def stA0(tl=tl, zb=zb, n=n):
                        op(PE, lambda: TE.matmul(PS[zb][:M, :n], lhsT=qT_ap, rhs=tl["kT"], start=True, stop=True), r=[qn, tl["kn"]], w=[PSN[zb]])

                    def stA(tl=tl, zb=zb, n=n, B=B, Bn=Bn, src=src, srcn=srcn, fb=fb):
                        if tl["mask"] is not None:
                            op(DVE, lambda: V.tensor_tensor(out=B["zs"][:M, :n], in0=PS[zb][:M, :n], in1=tl["mask"], op=ALU.add),
                               r=[PSN[zb], "cst"], w=[Bn["zs"]])
                        if fb is not None:
                            op(ACT, lambda: S.activation(out=B["e"][:M, :n], in_=src, func=AF.Exp, bias=fb), r=[srcn, "flg"], w=[Bn["e"]])
                        else:
                            op(ACT, lambda: S.activation(out=B["e"][:M, :n], in_=src, func=AF.Exp), r=[srcn], w=[Bn["e"]])

                    def stA2(n=n, B=B, Bn=Bn):
                        op(ACT, lambda: S.activation(out=B["sp"][:M, :n], in_=B["e"][:M, :n], func=AF.Ln, bias=1.0), r=[Bn["e"]], w=[Bn["sp"]])

                    def stB(n=n, B=B, Bn=Bn, src=src, srcn=srcn, fb=fb, ti=ti, c0=c0, c1=c1, bias_ap=bias_ap, bn_=bn_):
                        op(DVE, lambda: V.tensor_tensor_scan(out=B["P"][:M, 1:n + 1], data0=ones512[:M, :n], data1=B["sp"][:M, :n], initial=0.0,
                                                             op0=ALU.mult, op1=ALU.add), r=[Bn["sp"], "ones512"], w=[Bn["P"]])

                    def stB2(n=n, B=B, Bn=Bn, src=src, srcn=srcn, fb=fb, ti=ti, c0=c0, c1=c1, bias_ap=bias_ap, bn_=bn_):
                        if fb is not None:
                            op(DVE, lambda: V.scalar_tensor_tensor(out=B["a"][:M, :n], in0=src, scalar=fb, in1=B["P"][:M, 0:n], op0=ALU.add, op1=ALU.add),
                               r=[srcn, Bn["P"], "flg"], w=[Bn["a"]])
                        else:
                            op(DVE, lambda: V.tensor_tensor(out=B["a"][:M, :n], in0=src, in1=B["P"][:M, 0:n], op=ALU.add), r=[srcn, Bn["P"]], w=[Bn["a"]])
                        op(DVE, lambda: V.tensor_tensor(out=nb[:M, ti + 1:ti + 2], in0=nb[:M, ti:ti + 1], in1=B["P"][:M, n:n + 1], op=ALU.subtract),
                           r=[Bn["P"], c0, "nb%d" % qi], w=[c1])

                    def stC(tl=tl, n=n, B=B, Bn=Bn, ti=ti, bias_ap=bias_ap, bn_=bn_, nsub=nsub, ks=ks, tv=tv, tb=tb, sl=sl):
                        op(ACT, lambda: S.activation(out=B["w"][:M, :n], in_=B["a"][:M, :n], func=AF.Exp, bias=bias_ap), r=[Bn["a"], bn_], w=[Bn["w"]])

                    def stC2(tl=tl, n=n, B=B, Bn=Bn, ti=ti, nsub=nsub, ks=ks, tv=tv, tb=tb, sl=sl):
                        def trw():
                            ins = None
                            for s_ in range(nsub):
                                ins = TE.transpose(tv[:ks, s_ * 128:s_ * 128 + M], B["w"][:M, s_ * 128:s_ * 128 + ks], ident_b[:M, :M])
                            return ins
                        op(PE, trw, r=[Bn["w"], "ident_b"], w=[PSN[tb]])

                    def stC3(tl=tl, n=n, B=B, Bn=Bn, ti=ti, nsub=nsub, ks=ks, tv=tv, tb=tb, sl=sl):
                        wTv = B["wT"][:ks, :nsub, :M]
                        tvv = tv[:ks, :nsub * 128].rearrange("p (s q) -> p s q", q=128)[:, :, :M]
                        op(ACT, lambda: S.copy(out=wTv, in_=tvv), r=[PSN[tb]], w=[Bn["wT"]])

                        def wv():
                            ins = None
                            for s_ in range(nsub):
                                ins = TE.matmul(PS[ob][pb:pb + 64, :M], lhsT=tl["vsub"](s_), rhs=B["wT"][:ks, s_, :M],
                                                start=(ti == 0 and s_ == 0), stop=(ti == nt - 1 and s_ == nsub - 1))
                            return ins
                        op(PE, wv, r=[Bn["wT"], tl["vn"]], w=[PSN[ob]])
                        if ti == nt - 1:
                            op(ACT, lambda: S.copy(out=dst_ap, in_=PS[ob][pb:pb + 64, :M]), r=[PSN[ob]], w=[dstn])
                    push(dict(A0=stA0, A1=stA, A2=stA2, B1=stB, B2=stB2, C1=stC, C2=stC2, C3=stC3))

            NQT = int(os.environ.get("KNQT", "16"))
            NHP = int(os.environ.get("KNHP", "4"))
            for hp in range(NHP):
                i = hp % 2
                kTb, vbb, qb = bufA[i], bufB[i], qTb[i]
                nk = (j + 1) * TOWN
                dma("sp", kTb[:, 0:nk], kT_scr[hp][:, 0:nk], r=["kT_scr"], w=["bufA%d" % i])
                dma("sp", vbb[:, 0:nk], v_scr[hp][:, 0:nk], r=["v_scr"], w=["bufB%d" % i])
                dma("sp", qb, qT_scr[hp], r=["qT_scr"], w=["qTb%d" % i])
                vb3 = vbb.rearrange("p (t c) -> p t c", c=128)
                for hh in range(2):
                    pb = hh * 64
                    for qt in range(NQT):
                        tiles = []
                        for kt in range(qt // 4, -1, -1):
                            key0 = j * TOWN + kt * 512
                            diag = (kt == qt // 4)
                            m_ = qt % 4
                            tiles.append(dict(kT=kTb[pb:pb + 64, key0:key0 + 512], kn="bufA%d" % i, n=512,
                                              vsub=(lambda s_, key0=key0, vb3=vb3, pb=pb: vb3[:, key0 // 128 + s_, pb:pb + 64]), vn="bufB%d" % i,
                                              mask=(cst[:, C_MASK + m_ * 512:C_MASK + (m_ + 1) * 512] if diag else None), fb=None))
                        for kt in range(4 * j - 1, 4 * NSLOT0 - 1, -1):
                            key0 = kt * 512
                            jj = kt // 4
                            tiles.append(dict(kT=kTb[pb:pb + 64, key0:key0 + 512], kn="bufA%d" % i, n=512,
                                              vsub=(lambda s_, key0=key0, vb3=vb3, pb=pb: vb3[:, key0 // 128 + s_, pb:pb + 64]), vn="bufB%d" % i,
                                              mask=None, fb=flg[:, 3 + jj:4 + jj]))
                        att_qtile(128, qb[pb:pb + 64, qt * 128:(qt + 1) * 128], "qTb%d" % i, tiles, pb,
                                  oaT[pb:pb + 64, hp, qt * 128:(qt + 1) * 128], "oaT")
            flush()
            for s_ in range(2 if (NHP == 4 and is_own) else 0):
                for hp in range(4):
                    ckb = bufA[0][:, 0:4096].rearrange("p (t c) -> p t c", c=128)
                    kTsb = bufA[1][:, 0:4096]
                    vsb = bufB[0][:, 0:4096].rearrange("p (t c) -> p t c", c=128)
                    dma("pool", ckb, cache_k[s_].rearrange("(t p) c -> p t c", p=128)[:, :, hp * 128:(hp + 1) * 128], w=["bufA0"])
                    dma("pool", vsb, cache_v[s_].rearrange("(t p) c -> p t c", p=128)[:, :, hp * 128:(hp + 1) * 128], w=["bufB0"])
                    for rnd in range(4):
                        tb = 2 + rnd % 2
                        tv = PS[tb][:, :].bitcast(BF16)

                        def trc(rnd=rnd, tv=tv, ckb=ckb):
                            ins = None
                            for t8 in range(8):
                                ins = TE.transpose(tv[:, t8 * 128:(t8 + 1) * 128], ckb[:, rnd * 8 + t8, :], ident_b)
                            return ins
                        op(PE, trc, r=["bufA0", "ident_b"], w=[PSN[tb]])
                        if rnd % 2 == 0:
                            op(DVE, lambda rnd=rnd, tv=tv, kTsb=kTsb: V.tensor_copy(out=kTsb[:, rnd * 1024:(rnd + 1) * 1024], in_=tv), r=[PSN[tb]], w=["bufA1"])
                        else:
                            op(ACT, lambda rnd=rnd, tv=tv, kTsb=kTsb: S.copy(out=kTsb[:, rnd * 1024:(rnd + 1) * 1024], in_=tv), r=[PSN[tb]], w=["bufA1"])
                    for hh in range(2):
                        pb = hh * 64
                        h = 2 * hp + hh
                        tiles = [dict(kT=kTs[pb:pb + 64, hp, s_ * 64:(s_ + 1) * 64], kn="kTs", n=64,
                                      vsub=(lambda s2, h=h, s_=s_: vs_new[0:64, s_, h * 64:(h + 1) * 64]), vn="vs_new",
                                      mask=cst[0:64, C_MASK:C_MASK + 64], fb=None)]
                        for kt in range(7, -1, -1):
                            tiles.append(dict(kT=kTsb[pb:pb + 64, kt * 512:(kt + 1) * 512], kn="bufA1", n=512,
                                              vsub=(lambda s2, kt=kt, pb=pb, vsb=vsb: vsb[:, kt * 4 + s2, pb:pb + 64]), vn="bufB0", mask=None, fb=None))
                        att_qtile(64, qTs[pb:pb + 64, hp, s_ * 64:(s_ + 1) * 64], "qTs", tiles, pb,
                                  oaT[pb:pb + 64, hp, TOWN + s_ * 64:TOWN + (s_ + 1) * 64], "oaT")
                flush()
            T.barrier()

        xs = ExitStack()
        xT = sb(xs, "xT", [128, 8, TOWN + NSMP], F32)
        FN["xT"] = xT

        def dump_x(tag):
            if os.environ.get("KDBG") == tag:
                dma("sp", dbg["x"], xT.rearrange("p k t -> p (k t)"), r=["xT"], w=["dbg_x"])

        FN["dump_x"] = dump_x

        def mixer_out(W, Wn, srcs, l, ph):
            ranges = [(t * 512, (t + 1) * 512, 0) for t in range(TOWN // 512)] + ([(TOWN, TOWN + 64, 1), (TOWN + 64, TOWN + 128, 2)] if is_own else [])
            ctr = 0
            for (c0, c1, b) in ranges:
                n = c1 - c0
                for m in range(8):
                    pi = ctr % 2; ctr += 1
                    rr = [Wn, "xsrc"]

                    def f(m=m, c0=c0, c1=c1, pi=pi, n=n):
                        ins = None
                        for kc in range(8):
                            ins = TE.matmul(PS[pi][:, :n], lhsT=W[:, kc, m * 128:(m + 1) * 128], rhs=srcs(kc, c0, c1), start=(kc == 0), stop=(kc == 7))
                        return ins
                    op(PE, f, r=rr, w=[PSN[pi]])
                    op(DVE, lambda m=m, c0=c0, c1=c1, pi=pi, n=n, b=b: V.scalar_tensor_tensor(
                        out=xT[:, m, c0:c1], in0=PS[pi][:, :n], scalar=MOD(l, 2, m, b), in1=xT[:, m, c0:c1], op0=ALU.mult, op1=ALU.add),
                        r=[PSN[pi], "mod", "xT"], w=["xT"])

        if STAGE >= 3:
          with ExitStack() as ph:
            woe = sb(ph, "woe", [128, 8, D], BF16)
            if "woe" not in WCACHED:
                dma("pool", woe, w_out_even.rearrange("(k p) c -> p k c", p=128), w=["woe"])
                dma("sp", woe_s.rearrange("p (k c) -> p k c", c=D), woe, r=["woe"], w=["woe_s"])
                WCACHED.add("woe")
            else:
                dma("sp", woe, woe_s.rearrange("p (k c) -> p k c", c=D), r=["woe_s"], w=["woe"])
            for t in range(TOWN // 512):
                load_xT(xsrc_rows[t * 512:(t + 1) * 512, :], 4, xT, "xT", t * 512, 2, 3)
            if is_own:
                load_xT(x_smp, 1, xT, "xT", TOWN, 2, 3)
            op(POOL, lambda: G.memset(tmpv[:, 60:61], 0.0), r=["oaT", "obT"], w=["xsrc"])

            def srcs0(kc, c0, c1):
                return oaT[:, kc, c0:c1] if kc < 4 else obT[:, kc - 4, c0:c1]
            mixer_out(woe, "woe", srcs0, 0, ph)
            dump_x("xa0")
            T.barrier()


        T.barrier()
        ls.close()

        def ffn(l, with_smp=True):
            with ExitStack() as ph:
                hn2 = sb(ph, "hn2", [128, 8, 1088], BF16)
                aT = sb(ph, "aT", [128, 22, 1088], BF16)
                wgb = [sb(ph, "wgb%d" % i, [128, 8, 256], BF16) for i in range(2)]
                wub = [sb(ph, "wub%d" % i, [128, 8, 256], BF16) for i in range(2)]
                wdb = [sb(ph, "wdb%d" % i, [128, 22, 256], BF16) for i in range(2)]
                sgl = ntmp
                wgv = ffn_wg[l].rearrange("(k p) c -> p k c", p=128)
                wuv = ffn_wu[l].rearrange("(k p) c -> p k c", p=128)
                wdv = ffn_wd[l].rearrange("(c p) m -> p c m", p=128)
                gctr = 0
                dctr = 0
                ectr = 0
                for half in range(2):
                    ranges = [(0, 512, half * 1024, 0), (512, 1024, half * 1024 + 512, 0)] + ([(1024, 1088, TOWN + half * 64, 1 + half)] if with_smp else [])
                    for (lc0, lc1, xc0, b) in ranges:
                        n = lc1 - lc0
                        norm_tile(xT[:, :, xc0:xc0 + n], "xT", n, [(0, n, l, b)], hn2[:, :, lc0:lc1], "hn2", 6, 3, 4)
                    for g in range(11):
                        gi = gctr % 2; gctr += 1
                        ck = ("ffn", l, "gu", g)
                        if ck not in WCACHED:
                            dma("pool", wgb[gi], wgv[:, :, g * 256:(g + 1) * 256], r=["wg_s%d" % g], w=["wgb%d" % gi])
                            dma("pool", wub[gi], wuv[:, :, g * 256:(g + 1) * 256], r=["wu_s%d" % g], w=["wub%d" % gi])
                            dma("sp", wg_s[g].rearrange("p (k c) -> p k c", c=256), wgb[gi], r=["wgb%d" % gi], w=["wg_s%d" % g])
                            dma("sp", wu_s[g].rearrange("p (k c) -> p k c", c=256), wub[gi], r=["wub%d" % gi], w=["wu_s%d" % g])
                            WCACHED.add(ck)
                        else:
                            dma("sp", wgb[gi], wg_s[g].rearrange("p (k c) -> p k c", c=256), r=["wg_s%d" % g], w=["wgb%d" % gi])
                            dma("sp", wub[gi], wu_s[g].rearrange("p (k c) -> p k c", c=256), r=["wu_s%d" % g], w=["wub%d" % gi])
                        for i2 in range(2):
                            c = 2 * g + i2
                            for (lc0, lc1, xc0, b) in ranges:
                                n = lc1 - lc0
                                e2 = ectr % 2; ectr += 1
                                pa, pbk = e2, 2 + e2

                                def fg(wt, pi, i2=i2, lc0=lc0, lc1=lc1, n=n):
                                    ins = None
                                    for k in range(8):
                                        ins = TE.matmul(PS[pi][:, :n], lhsT=wt[:, k, i2 * 128:(i2 + 1) * 128], rhs=hn2[:, k, lc0:lc1], start=(k == 0), stop=(k == 7))
                                    return ins
                                op(PE, lambda fg=fg, gi=gi, pa=pa: fg(wgb[gi], pa), r=["wgb%d" % gi, "hn2"], w=[PSN[pa]])
                                op(PE, lambda fg=fg, gi=gi, pbk=pbk: fg(wub[gi], pbk), r=["wub%d" % gi, "hn2"], w=[PSN[pbk]])
                                sg_ = sgl[e2]; sgn = "ntmp%d" % e2
                                op(ACT, lambda sg_=sg_, pa=pa, n=n: S.activation(out=sg_[:, :n], in_=PS[pa][:, :n], func=AF.Silu), r=[PSN[pa]], w=[sgn])
                                op(DVE, lambda sg_=sg_, pbk=pbk, n=n, c=c, lc0=lc0, lc1=lc1: V.tensor_tensor(out=aT[:, c, lc0:lc1], in0=sg_[:, :n], in1=PS[pbk][:, :n], op=ALU.mult),
                                   r=[sgn, PSN[pbk]], w=["aT"])
                    for mp in range(4):
                        di = dctr % 2; dctr += 1
                        ck = ("ffn", l, "d", mp)
                        if ck not in WCACHED:
                            dma("pool", wdb[di], wdv[:, :, mp * 256:(mp + 1) * 256], r=["wd_s%d" % mp], w=["wdb%d" % di])
                            dma("sp", wd_s[mp].rearrange("p (k c) -> p k c", c=256), wdb[di], r=["wdb%d" % di], w=["wd_s%d" % mp])
                            WCACHED.add(ck)
                        else:
                            dma("sp", wdb[di], wd_s[mp].rearrange("p (k c) -> p k c", c=256), r=["wd_s%d" % mp], w=["wdb%d" % di])
                        for i2 in range(2):
                            m = 2 * mp + i2
                            for (lc0, lc1, xc0, b) in ranges:
                                n = lc1 - lc0
                                e2 = ectr % 2; ectr += 1
                                po = 4 + e2

                                def fd(di=di, i2=i2, lc0=lc0, lc1=lc1, n=n, po=po):
                                    ins = None
                                    for c in range(22):
                                        ins = TE.matmul(PS[po][:, :n], lhsT=wdb[di][:, c, i2 * 128:(i2 + 1) * 128], rhs=aT[:, c, lc0:lc1], start=(c == 0), stop=(c == 21))
                                    return ins
                                op(PE, fd, r=["wdb%d" % di, "aT"], w=[PSN[po]])
                                op(DVE, lambda m=m, xc0=xc0, n=n, po=po, b=b: V.scalar_tensor_tensor(
                                    out=xT[:, m, xc0:xc0 + n], in0=PS[po][:, :n], scalar=MOD(l, 5, m, b), in1=xT[:, m, xc0:xc0 + n], op0=ALU.mult, op1=ALU.add),
                                    r=[PSN[po], "mod", "xT"], w=["xT"])
                T.barrier()

        FN["ffn"] = ffn
        if STAGE >= 4:
            ffn(0, is_own)
            if is_own:
                dump_x("xb0")

        if STAGE >= 5:
          with ExitStack() as ph:
            wio = sb(ph, "wio", [128, 8, 2048], BF16)
            wv_ = w_in_odd.rearrange("(k p) c -> p k c", p=128)
            wio_s3 = wio_s.rearrange("p (k c) -> p k c", c=2048)
            for g in range(4):
                if "wio" not in WCACHED:
                    dma("pool", wio[:, :, g * 512:(g + 1) * 512], wv_[:, :, g * 512:(g + 1) * 512], w=["wio%d" % g])
                    dma("sp", wio_s3[:, :, g * 512:(g + 1) * 512], wio[:, :, g * 512:(g + 1) * 512], r=["wio%d" % g], w=["wio_s"])
                else:
                    dma("sp", wio[:, :, g * 512:(g + 1) * 512], wio_s3[:, :, g * 512:(g + 1) * 512], r=["wio_s"], w=["wio%d" % g])
            WCACHED.add("wio")
            waw = sb(ph, "waw", [128, 8, 128], BF16)
            wxw = sb(ph, "wxw", [128, 8, 128], BF16)
            dma("pool", waw, lru_wa.rearrange("h i j -> i h j"), w=["waw"])
            dma("pool", wxw, lru_wx.rearrange("h i j -> i h j"), w=["wxw"])
            woo = sb(ph, "woo", [128, 8, D], BF16)
            if is_own:
                dma("pool", woo, w_out_odd.rearrange("(k p) c -> p k c", p=128), w=["woo"])
            hn1 = sb(ph, "hn1", [128, 8, 512], BF16)
            yT = sb(ph, "yT", [128, 8, 512], BF16)
            LB = {}
            for nm in ("xbr",):
                LB[nm] = [sb(ph, "l_%s%d" % (nm, i), [128, 520], F32) for i in range(2)]
            for nm in ("xc", "rr", "gi", "aa", "uu", "hh", "gg"):
                LB[nm] = [sb(ph, "l_%s%d" % (nm, i), [128, 512], F32) for i in range(2)]
            LB["xcb"] = [sb(ph, "l_xcb%d" % i, [128, 512], BF16) for i in range(2)]
            f0v = sb(ph, "f0v", [128, 2], F32)
            f0c = flg[:, 24 + j:25 + j]
            op(DVE, lambda: V.tensor_scalar(out=f0v[:, 0:1], in0=f0c, scalar1=-1.0, scalar2=1.0, op0=ALU.mult, op1=ALU.add), r=["flg"], w=["f0v"])
            lctr = [0]

            def l1_tile(c0, n, b, p, sq_, first):
                norm_tile(xT[:, :, c0:c0 + n], "xT", n, [(0, n, 1, b)], hn1, "hn1", 6, 0, 1)
                l_lists = []
                for h in range(8):
                    cur = []
                    l_lists.append(cur)
                    OP[0] = (lambda eng, fn, r=(), w=(), cur=cur: cur.append((eng, fn, r, w)))
                    i2 = lctr[0] % 2; lctr[0] += 1
                    B = {k_: LB[k_][i2] for k_ in LB}
                    Bn = {k_: "l_%s%d" % (k_, i2) for k_ in LB}
                    pa = i2

                    def fx(h=h, pa=pa, col0=1024):
                        ins = None
                        for k in range(8):
                            ins = TE.matmul(PS[pa][:, :n], lhsT=wio[:, k, col0 + h * 128:col0 + (h + 1) * 128], rhs=hn1[:, k, :n], start=(k == 0), stop=(k == 7))
                        return ins
                    op(PE, fx, r=["hn1", "wio%d" % ((1024 + h * 128) // 512)], w=[PSN[pa]])
                    op(ACT, lambda B=B, pa=pa: S.copy(out=B["xbr"][:, 3:3 + n], in_=PS[pa][:, :n]), r=[PSN[pa]], w=[Bn["xbr"]])
                    hs = hist[:, sq_, h * 3:(h + 1) * 3]
                    hn_ = "hist%d_%d" % (sq_, h)
                    op(POOL, lambda B=B, hs=hs: G.tensor_copy(out=B["xbr"][:, 0:3], in_=hs), r=[hn_, "hist"], w=[Bn["xbr"] + "h"])
                    xr = [Bn["xbr"], Bn["xbr"] + "h"]
                    op(DVE, lambda B=B, h=h: V.tensor_scalar(out=B["xc"][:, :n], in0=B["xbr"][:, 0:n], scalar1=vec[:, V_CW + h:V_CW + h + 1], scalar2=vec[:, V_CB + h:V_CB + h + 1],
                                                             op0=ALU.mult, op1=ALU.add), r=xr + ["vec"], w=[Bn["xc"]])
                    for j in range(1, 4):
                        op(DVE, lambda B=B, h=h, j=j: V.scalar_tensor_tensor(out=B["xc"][:, :n], in0=B["xbr"][:, j:j + n], scalar=vec[:, V_CW + j * 8 + h:V_CW + j * 8 + h + 1],
                                                                             in1=B["xc"][:, :n], op0=ALU.mult, op1=ALU.add), r=xr + [Bn["xc"], "vec"], w=[Bn["xc"]])
                    op(POOL, lambda B=B, hs=hs: G.tensor_copy(out=hs, in_=B["xbr"][:, n:n + 3]), r=xr, w=[hn_])
                    op(ACT, lambda B=B: S.copy(out=B["xcb"][:, :n], in_=B["xc"][:, :n]), r=[Bn["xc"]], w=[Bn["xcb"]])
                    pr, pg = 2 + i2, 4 + i2
                    op(PE, lambda B=B, h=h, pr=pr: TE.matmul(PS[pr][:, :n], lhsT=waw[:, h, :], rhs=B["xcb"][:, :n], start=True, stop=True), r=[Bn["xcb"], "waw"], w=[PSN[pr]])
                    op(PE, lambda B=B, h=h, pg=pg: TE.matmul(PS[pg][:, :n], lhsT=wxw[:, h, :], rhs=B["xcb"][:, :n], start=True, stop=True), r=[Bn["xcb"], "wxw"], w=[PSN[pg]])
                    op(ACT, lambda B=B, h=h, pr=pr: S.activation(out=B["rr"][:, :n], in_=PS[pr][:, :n], func=AF.Sigmoid, bias=vec[:, V_BA + h:V_BA + h + 1]),
                       r=[PSN[pr], "vec"], w=[Bn["rr"]])
                    op(ACT, lambda B=B, h=h, pg=pg: S.activation(out=B["gi"][:, :n], in_=PS[pg][:, :n], func=AF.Sigmoid, bias=vec[:, V_BX + h:V_BX + h + 1]),
                       r=[PSN[pg], "vec"], w=[Bn["gi"]])
                    op(ACT, lambda B=B, h=h: S.activation(out=B["aa"][:, :n], in_=B["rr"][:, :n], func=AF.Exp, scale=clv[:, h:h + 1]), r=[Bn["rr"], "clv"], w=[Bn["aa"]])
                    op(DVE, lambda B=B: V.tensor_tensor(out=B["uu"][:, :n], in0=B["aa"][:, :n], in1=B["aa"][:, :n], op=ALU.mult), r=[Bn["aa"]], w=[Bn["uu"]])
                    op(ACT, lambda B=B: S.activation(out=B["uu"][:, :n], in_=B["uu"][:, :n], func=AF.Sqrt, scale=-1.0, bias=1.0), r=[Bn["uu"]], w=[Bn["uu"]])
                    if first:
                        op(DVE, lambda B=B: V.tensor_scalar(out=B["uu"][:, 0:1], in0=B["uu"][:, 0:1], scalar1=f0v[:, 0:1], scalar2=f0c, op0=ALU.mult, op1=ALU.add),
                           r=[Bn["uu"], "f0v", "flg"], w=[Bn["uu"]])
                    op(DVE, lambda B=B: V.tensor_tensor(out=B["uu"][:, :n], in0=B["uu"][:, :n], in1=B["gi"][:, :n], op=ALU.mult), r=[Bn["uu"], Bn["gi"]], w=[Bn["uu"]])
                    op(POOL, lambda B=B: G.tensor_tensor(out=B["uu"][:, :n], in0=B["uu"][:, :n], in1=B["xc"][:, :n], op=ALU.mult), r=[Bn["uu"], Bn["xc"]], w=[Bn["uu"]])
                    cr = carry[:, sq_, h:h + 1]
                    cn = "carry%d_%d" % (sq_, h)
                    op(DVE, lambda B=B, cr=cr: V.tensor_tensor_scan(out=B["hh"][:, :n], data0=B["aa"][:, :n], data1=B["uu"][:, :n], initial=cr, op0=ALU.mult, op1=ALU.add),
                       r=[Bn["aa"], Bn["uu"], cn, "carry"], w=[Bn["hh"]])
                    op(POOL, lambda B=B, cr=cr: G.tensor_copy(out=cr, in_=B["hh"][:, n - 1:n]), r=[Bn["hh"]], w=[cn])
                    if p == 2:
                        pq = 6 + i2
                        op(PE, lambda fx=fx, pq=pq: fx(pa=pq, col0=0), r=["hn1", "wio%d" % ((h * 128) // 512)], w=[PSN[pq]])
                        op(ACT, lambda B=B, pq=pq: S.activation(out=B["gg"][:, :n], in_=PS[pq][:, :n], func=AF.Square), r=[PSN[pq]], w=[Bn["gg"]])
                        op(DVE, lambda B=B: V.tensor_scalar(out=B["gg"][:, :n], in0=B["gg"][:, :n], scalar1=0.044715, scalar2=1.0, op0=ALU.mult, op1=ALU.add), r=[Bn["gg"]], w=[Bn["gg"]])
                        op(DVE, lambda B=B, pq=pq: V.tensor_tensor(out=B["gg"][:, :n], in0=B["gg"][:, :n], in1=PS[pq][:, :n], op=ALU.mult), r=[Bn["gg"], PSN[pq]], w=[Bn["gg"]])
                        op(ACT, lambda B=B: S.activation(out=B["gg"][:, :n], in_=B["gg"][:, :n], func=AF.Sigmoid, scale=1.5957691216057308), r=[Bn["gg"]], w=[Bn["gg"]])
                        op(DVE, lambda B=B, pq=pq: V.tensor_tensor(out=B["gg"][:, :n], in0=B["gg"][:, :n], in1=PS[pq][:, :n], op=ALU.mult), r=[Bn["gg"], PSN[pq]], w=[Bn["gg"]])
                        op(POOL, lambda B=B, h=h: G.tensor_tensor(out=yT[:, h, :n], in0=B["gg"][:, :n], in1=B["hh"][:, :n], op=ALU.mult), r=[Bn["gg"], Bn["hh"]], w=["yT"])
                OP[0] = T.op
                for h0 in range(0, 8, 2):
                    la, lb_ = l_lists[h0], l_lists[h0 + 1]
                    for i_ in range(max(len(la), len(lb_))):
                        if i_ < len(la):
                            T.op(*la[i_])
                        if i_ < len(lb_):
                            T.op(*lb_[i_])
                if p == 2:
                    for m in range(8):
                        pi = m % 2

                        def fo(m=m, pi=pi):
                            ins = None
                            for h in range(8):
                                ins = TE.matmul(PS[pi][:, :n], lhsT=woo[:, h, m * 128:(m + 1) * 128], rhs=yT[:, h, :n], start=(h == 0), stop=(h == 7))
                            return ins
                        op(PE, fo, r=["yT", "woo"], w=[PSN[pi]])
                        op(DVE, lambda m=m, pi=pi: V.scalar_tensor_tensor(out=xT[:, m, c0:c0 + n], in0=PS[pi][:, :n], scalar=MOD(1, 2, m, b), in1=xT[:, m, c0:c0 + n],
                                                                          op0=ALU.mult, op1=ALU.add), r=[PSN[pi], "mod", "xT"], w=["xT"])

            allh = ["hist%d_%d" % (q_, h) for q_ in range(3) for h in range(8)] + ["hist"]
            allc = ["carry%d_%d" % (q_, h) for q_ in range(3) for h in range(8)] + ["carry"]
            if j == NSLOT0:
                op(DVE, lambda: V.memset(hist, 0.0), w=["hist"])
                op(DVE, lambda: V.memset(carry, 0.0), w=["carry"])
            if is_own:
                for s_ in range(2):
                    dma("sp", hist[:, 1 + s_, :], st_conv[s_], r=["hist"], w=["hists%d" % s_])
                    dma("sp", carry[:, 1 + s_, :], st_lru[s_], r=["carry"], w=["carrys%d" % s_])
                op(POOL, lambda: G.memset(f0v[:, 1:2], 0.0), r=["hists0", "hists1", "carrys0", "carrys1", "hist", "carry"], w=["hist", "carry"])
            for t in range(4):
                l1_tile(t * 512, 512, 0, 2 if is_own else 1, 0, t == 0)
            if is_own:
                for s_ in range(2):
                    l1_tile(TOWN + s_ * 64, 64, 1 + s_, 2, 1 + s_, False)
                dma("sp", conv_p, hist[:, 0, :], r=allh, w=["conv_p"])
                dma("sp", lru_p, carry[:, 0, :], r=allc, w=["lru_p"])
                for s_ in range(2):
                    dma("sp", conv_s[s_], hist[:, 1 + s_, :], r=allh, w=["conv_s"])
                    dma("sp", lru_s[s_], carry[:, 1 + s_, :], r=allc, w=["lru_s"])
            else:
                op(DVE, lambda: V.tensor_scalar(out=hist[:, 0, :], in0=hist[:, 0, :], scalar1=flg[:, j:j + 1], scalar2=None, op0=ALU.mult), r=allh + ["flg"], w=["hist"])
                op(DVE, lambda: V.tensor_scalar(out=carry[:, 0, :], in0=carry[:, 0, :], scalar1=flg[:, j:j + 1], scalar2=None, op0=ALU.mult), r=allc + ["flg"], w=["carry"])
            if is_own:
                dump_x("xa1")
            T.barrier()
        if not is_own:
            xs.close()

    for j_ in range(NSLOT0, 4):
        run_slot(j_, j_ == 3)
    T.barrier()
    l0.close()
    ffn = FN["ffn"]
    xT = FN["xT"]
    dump_x = FN["dump_x"]

    if STAGE >= 6:
        ffn(1, True)
        dump_x("xb1")

    if STAGE >= 7:
      with ExitStack() as ph:
        ytm = sb(ph, "ytm", [128, 8, 512], F32)
        ysb = [sb(ph, "ysb%d" % i, [128, D], F32) for i in range(2)]
        yctr = 0
        tl_ = [(t * 512, 512, y_own, t * 512) for t in range(4)] + [(TOWN, 128, y_smp, 0)]
        for (c0, n, ydst, row0) in tl_:
            for k in range(8):
                q = sq[k % 2]; qn = "sq%d" % (k % 2)
                op(ACT, lambda k=k, q=q: S.activation(out=q[:, :n], in_=xT[:, k, c0:c0 + n], func=AF.Square), r=["xT"], w=[qn])
                op(PE, lambda k=k, q=q: TE.matmul(PS[6][:, :n], lhsT=od1024, rhs=q[:, :n], start=(k == 0), stop=(k == 7)), r=[qn, "od1024"], w=[PSN[6]])
            op(ACT, lambda: S.activation(out=rt[:, :n], in_=PS[6][:, :n], func=AF.Sqrt, bias=EPS), r=[PSN[6]], w=["rt"])
            op(DVE, lambda: V.reciprocal(out=rstd[:, :n], in_=rt[:, :n]), r=["rt"], w=["rstd"])
            for k in range(8):
                op(DVE, lambda k=k: V.scalar_tensor_tensor(out=ytm[:, k, :n], in0=xT[:, k, c0:c0 + n], scalar=vec[:, V_FN + k:V_FN + k + 1], in1=rstd[:, :n],
                                                           op0=ALU.mult, op1=ALU.mult), r=["xT", "rstd", "vec"], w=["ytm"])
            for s_ in range(n // 128):
                yb = ysb[yctr % 2]; yn = "ysb%d" % (yctr % 2); yctr += 1
                for half in range(2):
                    pi = half

                    def trf(s_=s_, half=half, pi=pi):
                        ins = None
                        for kk in range(4):
                            ins = TE.transpose(PS[pi][:, kk * 128:(kk + 1) * 128], ytm[:, half * 4 + kk, s_ * 128:(s_ + 1) * 128], ident_f)
                        return ins
                    op(PE, trf, r=["ytm", "cstA"], w=[PSN[pi]])
                    if half == 0:
                        op(ACT, lambda yb=yb, pi=pi: S.copy(out=yb[:, 0:512], in_=PS[pi][:, :]), r=[PSN[pi]], w=[yn])
                    else:
                        op(DVE, lambda yb=yb, pi=pi: V.tensor_copy(out=yb[:, 512:1024], in_=PS[pi][:, :]), r=[PSN[pi]], w=[yn])
                dma("sp", ydst[row0 + s_ * 128:row0 + (s_ + 1) * 128, :], yb, r=[yn], w=["y_out"])
        T.barrier()

    T.barrier()
    es.close()
    print("instructions:", T.ninst, "sems:", T.nsem)
    return nc


def _prep_inputs(inp):
    f32 = np.float32
    g = lambda k: np.ascontiguousarray(np.asarray(inp[k], dtype=f32))
    xP, xS = g("x_prompt"), g("x_sample")
    cP, cS = g("c_prompt"), g("c_sample")
    ck, cv = g("cache_sb_k")[0].reshape(16, PAST, 512), g("cache_sb_v")[0].reshape(16, PAST, 512)
    sth, stc, stl = g("state_hgrn")[0], g("state_conv")[0], g("state_lru")[0]

    def fm(v, nchunk):
        return np.ascontiguousarray(v.reshape(nchunk, 128).T)

    vec = np.zeros((128, NV), f32)
    nm, nf, ba = g("norm_mix"), g("norm_ffn"), g("b_ada")
    for l in range(2):
        vec[:, V_NMIX + l * 8:V_NMIX + l * 8 + 8] = fm(nm[l], 8)
        vec[:, V_NFFN + l * 8:V_NFFN + l * 8 + 8] = fm(nf[l], 8)
        vec[:, V_BADA + l * 48:V_BADA + (l + 1) * 48] = fm(ba[l], 48)
    vec[:, V_GN:V_GN + 4] = fm(g("hg_gnorm")[0], 4)
    lg = g("hg_lb_logits")
    for i in range(3):
        vec[:, V_LB + i * 4:V_LB + i * 4 + 4] = fm(lg[i], 4)
    cw = g("conv_w")[0]
    for j in range(4):
        vec[:, V_CW + j * 8:V_CW + j * 8 + 8] = fm(cw[j], 8)
    vec[:, V_CB:V_CB + 8] = fm(g("conv_b")[0], 8)
    vec[:, V_BA:V_BA + 8] = fm(g("lru_ba")[0], 8)
    vec[:, V_BX:V_BX + 8] = fm(g("lru_bx")[0], 8)
    vec[:, V_LAM:V_LAM + 8] = fm(g("lru_lambda")[0], 8)
    vec[:, V_FN:V_FN + 8] = fm(g("final_norm"), 8)

    cst = np.zeros((128, NCST), f32)
    cst[:, C_ID:C_ID + 128] = np.eye(128, dtype=f32)
    q = np.arange(128)[:, None]
    k = np.arange(512)[None, :]
    for m in range(4):
        cst[:, C_MASK + m * 512:C_MASK + (m + 1) * 512] = np.where(k < 128 * m + q, 0.0, NEG)
    s_ = np.arange(64)[:, None]
    t_ = np.arange(64)[None, :]
    cst[0:64, C_CM:C_CM + 64] = (s_ <= t_).astype(f32)
    cst[:, C_RS:C_RS + 512] = (np.arange(512) % 64 != 0).astype(f32)[None, :]

    shared = {
        "w_ada": g("w_ada"), "w_in_even": g("w_in_even")[0], "w_out_even": g("w_out_even")[0],
        "w_in_odd": g("w_in_odd")[0], "lru_wa": g("lru_wa")[0], "lru_wx": g("lru_wx")[0], "w_out_odd": g("w_out_odd")[0],
        "ffn_wg": g("ffn_wg"), "ffn_wu": g("ffn_wu"), "ffn_wd": g("ffn_wd"), "cst": cst,
    }
    in_maps = []
    for c in range(NCORE):
        b, r = c // 4, c % 4
        m = dict(shared)
        m["x_own"] = np.ascontiguousarray(xP[b, r * TOWN:(r + 1) * TOWN])
        xp = np.zeros((TPRE, D), f32)
        fl = np.zeros((128, 32), f32)
        for j in range(3):
            blk = r - 3 + j
            if blk >= 0:
                xp[j * TOWN:(j + 1) * TOWN] = xP[b, blk * TOWN:(blk + 1) * TOWN]
                fl[:, j] = 1.0
            else:
                fl[:, 3 + j] = NEG
        fl[:, 6] = 1.0 if r == 0 else 0.0
        fl[:, 24 + (3 - r)] = 1.0
        for i in range(NCORE):
            if i // 4 == b and i < c:
                fl[:, 8 + i] = 1.0
            if i == c - 1 and r > 0:
                fl[:, 16 + i] = 1.0
        m["x_pre"] = xp
        m["flags"] = fl
        m["x_smp"] = np.ascontiguousarray(xS[2 * c:2 * c + 2].reshape(NSMP, D))
        m["cache_k"] = np.ascontiguousarray(ck[2 * c:2 * c + 2])
        m["cache_v"] = np.ascontiguousarray(cv[2 * c:2 * c + 2])
        m["st_hgrn"] = np.ascontiguousarray(sth[2 * c:2 * c + 2])
        m["st_conv"] = np.ascontiguousarray(stc[2 * c:2 * c + 2].reshape(2, 3, 8, 128).transpose(0, 3, 2, 1).reshape(2, 128, 24))
        m["st_lru"] = np.ascontiguousarray(stl[2 * c:2 * c + 2].reshape(2, 8, 128).transpose(0, 2, 1))
        v = vec.copy()
        cs = [cP[b], cS[2 * c], cS[2 * c + 1]]
        for bi in range(3):
            cm = fm(cs[bi], 8)
            for kk in range(8):
                v[:, V_CT + kk * 4 + bi] = cm[:, kk]
        m["vec"] = v
        for kname, st in NEED_STAGE.items():
            if st > STAGE:
                m.pop(kname, None)
        in_maps.append(m)
    return in_maps


_NC = None
DBG_OUT = None


def kernel(**inputs):
    global _NC
    in_maps = _prep_inputs(inputs)
    if _NC is None:
        _NC = build()
    res = run_bass_kernel_spmd(_NC, in_maps, core_ids=list(range(NCORE)))
    R = res.results
    f32 = np.float32
    global DBG_OUT
    if os.environ.get("KDBG"):
        DBG_OUT = [R[c]["dbg_x"] for c in range(NCORE)]
    y_p = np.zeros((2, 8192, D), f32); k_p = np.zeros((1, 2, 8192, 8, 64), f32); v_p = np.zeros_like(k_p)
    y_s = np.zeros((16, 64, D), f32); k_s = np.zeros((1, 16, 64, 8, 64), f32); v_s = np.zeros_like(k_s)
    S_p = np.zeros((1, 2, 4, 128, 128), f32); conv_p = np.zeros((1, 2, 3, D), f32); h_p = np.zeros((1, 2, D), f32)
    S_s = np.zeros((1, 16, 4, 128, 128), f32); conv_s = np.zeros((1, 16, 3, D), f32); h_s = np.zeros((1, 16, D), f32)
    for c in range(NCORE):
        b, r = c // 4, c % 4
        o = R[c]
        o = {kk: o.get(kk, np.zeros(1, f32)) for kk in ("y_own","y_smp","k_own","v_own","S_p","conv_p","lru_p","k_smp","v_smp","S_s","conv_s","lru_s")} if STAGE < 9 else o
        sl = slice(r * TOWN, (r + 1) * TOWN)
        y_p[b, sl] = o["y_own"]
        k_p[0, b, sl] = o["k_own"].reshape(TOWN, 8, 64)
        v_p[0, b, sl] = o["v_own"].reshape(TOWN, 8, 64)
        y_s[2 * c:2 * c + 2] = o["y_smp"].reshape(2, 64, D)
        k_s[0, 2 * c:2 * c + 2] = o["k_smp"].reshape(2, 64, 8, 64)
        v_s[0, 2 * c:2 * c + 2] = o["v_smp"].reshape(2, 64, 8, 64)
        S_s[0, 2 * c:2 * c + 2] = o["S_s"]
        if STAGE >= 5:
            conv_s[0, 2 * c:2 * c + 2] = o["conv_s"].reshape(2, 128, 8, 3).transpose(0, 3, 2, 1).reshape(2, 3, D)
            h_s[0, 2 * c:2 * c + 2] = o["lru_s"].reshape(2, 128, 8).transpose(0, 2, 1).reshape(2, D)
        if r == 3:
            S_p[0, b] = o["S_p"]
            if STAGE >= 5:
                conv_p[0, b] = o["conv_p"].reshape(128, 8, 3).transpose(2, 1, 0).reshape(3, D)
                h_p[0, b] = o["lru_p"].reshape(128, 8).T.reshape(D)
    return (y_p, y_s, k_p, v_p, S_p, conv_p, h_p, k_s, v_s, S_s, conv_s, h_s)
```
